# Optimizing a Trainium2 kernel written in Bass

```python
import jax, jax.numpy as jnp
from jax import lax
import numpy as np

D_MODEL = 2048
BATCH = 2
SEQ = 8192
DEPTH = 1

GRID_W = 64
HEAD_DIM = 128
N_Q_HEADS = 16
N_KV_HEADS = 4
Q_BLOCK = 128
ROPE_THETA = 10000.0
ROPE_AXIS_DIM = HEAD_DIM // 2
HGRN_HEADS = D_MODEL // 128
HGRN_EXPAND = 128
HGRN_HEAD_V = D_MODEL // HGRN_HEADS
HGRN_CHUNK = 64
D_FF = 4 * D_MODEL
PLE_DIM = 256
EPS = 1e-6

ATTN_Q = N_Q_HEADS * HEAD_DIM
ATTN_KV = N_KV_HEADS * HEAD_DIM
HGRN_K = HGRN_HEADS * HGRN_EXPAND
HGRN_V = HGRN_HEADS * HGRN_HEAD_V
IN_SPLITS = (ATTN_Q, ATTN_KV, ATTN_KV, HGRN_K, HGRN_K, HGRN_K, HGRN_V, HGRN_V, D_MODEL, D_MODEL)
D_IN = ATTN_Q + 2 * ATTN_KV + 3 * HGRN_K + 2 * HGRN_V + 2 * D_MODEL

kernel_name = "hybrid_gqa_axialrope_hgrn2_bidir_gated_merge"


def rmsnorm(x, gain):
    xf = x.astype(jnp.float32)
    y = xf * lax.rsqrt(jnp.mean(xf * xf, axis=-1, keepdims=True) + EPS)
    return (y * gain.astype(jnp.float32)).astype(x.dtype)


def split_columns(proj):
    outs = []
    start = 0
    for width in IN_SPLITS:
        outs.append(proj[..., start:start + width])
        start += width
    return outs


def axial_rope_tables(seq_len):
    rows = seq_len // GRID_W
    row = jnp.repeat(jnp.arange(rows, dtype=jnp.float32), GRID_W)
    col = jnp.tile(jnp.arange(GRID_W, dtype=jnp.float32), rows)
    inv_freq = ROPE_THETA ** (-jnp.arange(0, ROPE_AXIS_DIM, 2, dtype=jnp.float32) / ROPE_AXIS_DIM)
    ang_row = row[:, None] * inv_freq[None, :]
    ang_col = col[:, None] * inv_freq[None, :]
    return (jnp.cos(ang_row), jnp.sin(ang_row), jnp.cos(ang_col), jnp.sin(ang_col))


def rotate_half_pairs(x, cos, sin):
    x1, x2 = jnp.split(x, 2, axis=-1)
    c = cos[None, :, None, :]
    s = sin[None, :, None, :]
    return jnp.concatenate([x1 * c - x2 * s, x1 * s + x2 * c], axis=-1)


def apply_axial_rope(x, tables):
    cos_r, sin_r, cos_c, sin_c = tables
    x_row, x_col = jnp.split(x, 2, axis=-1)
    out = jnp.concatenate([rotate_half_pairs(x_row, cos_r, sin_r),
                           rotate_half_pairs(x_col, cos_c, sin_c)], axis=-1)
    return out.astype(x.dtype)


def bidirectional_gqa(q, k, v):
    B, S = q.shape[0], q.shape[1]
    groups = N_Q_HEADS // N_KV_HEADS
    n_blocks = S // Q_BLOCK
    qb = q.reshape(B, n_blocks, Q_BLOCK, N_KV_HEADS, groups, HEAD_DIM).transpose(1, 0, 3, 4, 2, 5)
    scale = HEAD_DIM ** -0.5

    def one_block(q_blk):
        s = jnp.einsum('bhgqd,bkhd->bhgqk', q_blk, k).astype(jnp.float32) * scale
        w = jax.nn.softmax(s, axis=-1).astype(v.dtype)
        return jnp.einsum('bhgqk,bkhd->bqhgd', w, v)

    o = lax.map(one_block, qb)
    return jnp.moveaxis(o, 0, 1).reshape(B, S, N_Q_HEADS * HEAD_DIM)


def hgrn2_chunkwise(q, k, v, log_f):
    B, H, S, DK = q.shape
    DV = v.shape[-1]
    C = HGRN_CHUNK
    n_chunks = S // C

    def to_chunks(t):
        return jnp.moveaxis(t.reshape(B, H, n_chunks, C, t.shape[-1]), 2, 0)

    lower_tri = jnp.tril(jnp.ones((C, C), dtype=bool))[:, :, None]

    def step(state, inp):
        qc, kc, vc, ac = inp
        A = jnp.cumsum(ac, axis=-2)
        A_end = A[..., -1:, :]
        o_inter = jnp.einsum('bhtk,bhkv->bhtv', qc * jnp.exp(A), state)
        diff = A[..., :, None, :] - A[..., None, :, :]
        decay = jnp.exp(jnp.where(lower_tri, diff, -jnp.inf))
        scores = jnp.einsum('bhtk,bhsk,bhtsk->bhts', qc, kc, decay)
        o_intra = jnp.einsum('bhts,bhsv->bhtv', scores, vc)
        new_state = (jnp.exp(A_end[..., 0, :])[..., None] * state
                     + jnp.einsum('bhsk,bhsv->bhkv', kc * jnp.exp(A_end - A), vc))
        return new_state, o_inter + o_intra

    s0 = jnp.zeros((B, H, DK, DV), dtype=jnp.float32)
    _, o = lax.scan(step, s0, (to_chunks(q), to_chunks(k), to_chunks(v), to_chunks(log_f)))
    return jnp.moveaxis(o, 0, 2).reshape(B, H, S, DV)


def hgrn2_bidirectional(q_r, f_fwd, f_bwd, i_r, g_r, lb, g_norm):
    B, S = q_r.shape[0], q_r.shape[1]

    def heads(t, d):
        return t.astype(jnp.float32).reshape(B, S, HGRN_HEADS, d).transpose(0, 2, 1, 3)

    q = heads(jax.nn.silu(q_r), HGRN_EXPAND)
    v = heads(i_r, HGRN_HEAD_V)
    lbf = lb.astype(jnp.float32)

    def gate(f_raw, lower):
        f = lower + (1.0 - lower) * jax.nn.sigmoid(f_raw.astype(jnp.float32))
        return heads(1.0 - f, HGRN_EXPAND), heads(jnp.log(f), HGRN_EXPAND)

    k_fw, logf_fw = gate(f_fwd, lbf[0])
    k_bw, logf_bw = gate(f_bwd, lbf[1])
    o_fw = hgrn2_chunkwise(q, k_fw, v, logf_fw)
    rev = lambda t: jnp.flip(t, axis=2)
    o_bw = rev(hgrn2_chunkwise(rev(q), rev(k_bw), rev(v), rev(logf_bw)))
    o = (o_fw + o_bw).transpose(0, 2, 1, 3)
    o = rmsnorm(o, g_norm)
    gate_out = jax.nn.silu(g_r.astype(jnp.float32)).reshape(B, S, HGRN_HEADS, HGRN_HEAD_V)
    return (o * gate_out).reshape(B, S, HGRN_V).astype(q_r.dtype)


def setup_inputs(seed: int = 0) -> dict:
    key = jax.random.key(seed)
    ks = jax.random.split(key, 20)
    f32 = jnp.float32
    nrm = lambda k, shape, scale: jax.random.normal(k, shape, f32) * scale
    gain = lambda k, shape: 1.0 + 0.05 * jax.random.normal(k, shape, f32)
    return {
        "x": jax.random.normal(ks[0], (BATCH, SEQ, D_MODEL), f32),
        "p": jax.random.normal(ks[1], (DEPTH, BATCH, SEQ, PLE_DIM), f32),
        "g_mix": gain(ks[2], (DEPTH, D_MODEL)),
        "w_in": nrm(ks[3], (DEPTH, D_MODEL, D_IN), D_MODEL ** -0.5),
        "g_q": gain(ks[4], (DEPTH, HEAD_DIM)),
        "g_k": gain(ks[5], (DEPTH, HEAD_DIM)),
        "w_o_attn": nrm(ks[6], (DEPTH, ATTN_Q, D_MODEL), ATTN_Q ** -0.5),
        "hgrn_lb": nrm(ks[7], (DEPTH + 1, 2, HGRN_K), 0.5),
        "g_hgrn": gain(ks[8], (DEPTH, HGRN_HEAD_V)),
        "w_o_hgrn": nrm(ks[9], (DEPTH, HGRN_V, D_MODEL), HGRN_V ** -0.5),
        "w_out": nrm(ks[10], (DEPTH, D_MODEL, D_MODEL), D_MODEL ** -0.5),
        "g_mlp": gain(ks[11], (DEPTH, D_MODEL)),
        "w_up": nrm(ks[12], (DEPTH, D_MODEL, D_FF), D_MODEL ** -0.5),
        "w_down": nrm(ks[13], (DEPTH, D_FF, D_MODEL), D_FF ** -0.5),
        "g_ple": gain(ks[14], (DEPTH, D_MODEL)),
        "w_ple_gate": nrm(ks[15], (DEPTH, D_MODEL, D_MODEL), D_MODEL ** -0.5),
        "w_ple": nrm(ks[16], (DEPTH, PLE_DIM, D_MODEL), PLE_DIM ** -0.5),
        "g_final": gain(ks[17], (D_MODEL,)),
    }


def reference(x, p, g_mix, w_in, g_q, g_k, w_o_attn, hgrn_lb, g_hgrn, w_o_hgrn, w_out,
              g_mlp, w_up, w_down, g_ple, w_ple_gate, w_ple, g_final):
    B, S = x.shape[0], x.shape[1]
    rope_tables = axial_rope_tables(S)
    lb_all = jnp.cumsum(jax.nn.softmax(hgrn_lb.astype(jnp.float32), axis=0), axis=0)

    for i in range(DEPTH):
        h = rmsnorm(x, g_mix[i])
        proj = jnp.einsum('bsd,de->bse', h, w_in[i])
        q_a, k_a, v_a, q_r, f_fwd, f_bwd, i_r, g_r, gate_a, gate_r = split_columns(proj)

        q_a = apply_axial_rope(rmsnorm(q_a.reshape(B, S, N_Q_HEADS, HEAD_DIM), g_q[i]), rope_tables)
        k_a = apply_axial_rope(rmsnorm(k_a.reshape(B, S, N_KV_HEADS, HEAD_DIM), g_k[i]), rope_tables)
        v_a = v_a.reshape(B, S, N_KV_HEADS, HEAD_DIM)
        y_attn = jnp.einsum('bse,ed->bsd', bidirectional_gqa(q_a, k_a, v_a), w_o_attn[i])

        o_r = hgrn2_bidirectional(q_r, f_fwd, f_bwd, i_r, g_r, lb_all[i], g_hgrn[i])
        y_hgrn = jnp.einsum('bse,ed->bsd', o_r, w_o_hgrn[i])

        mixed = jax.nn.sigmoid(gate_a) * y_attn + jax.nn.sigmoid(gate_r) * y_hgrn
        x = x + jnp.einsum('bsd,de->bse', mixed, w_out[i])

        h = rmsnorm(x, g_mlp[i])
        u = jnp.square(jax.nn.relu(jnp.einsum('bsd,df->bsf', h, w_up[i])))
        x = x + jnp.einsum('bsf,fd->bsd', u, w_down[i])

        ple_gate = jax.nn.sigmoid(jnp.einsum('bsd,de->bse', rmsnorm(x, g_ple[i]), w_ple_gate[i]))
        x = x + ple_gate * jnp.einsum('bsc,cd->bsd', p[i].astype(x.dtype), w_ple[i])

    return rmsnorm(x, g_final)
```

```python
import numpy as np
import ml_dtypes
import concourse.bass as bass
import concourse.mybir as mybir
from concourse.bass_utils import run_bass_kernel_spmd

F32 = mybir.dt.float32
BF16 = mybir.dt.bfloat16
AF = mybir.ActivationFunctionType
ALU = mybir.AluOpType
AX = mybir.AxisListType

D = 2048
S = 8192
NOWN = 2048
HD = 128
NQH = 16
NKVH = 4
DFF = 8192
PLE = 256
EPS = 1e-6
DIN = 17408
C_Q, C_K, C_V, C_QR, C_FF, C_FB, C_I, C_G, C_GA, C_GR = 0, 2048, 2560, 3072, 5120, 7168, 9216, 11264, 13312, 15360


class Tk:
    __slots__ = ("w", "r")

    def __init__(self):
        self.w = None
        self.r = {}


class Stream:
    def __init__(self, name):
        self.name = name
        self.prog = []
        self.seen = {}


class Qu:
    def __init__(self, name, sems, stream, dma=False, pe=False):
        self.name = name
        self.sems = sems
        self.stream = stream
        self.dma = dma
        self.pe = pe
        self.n = 0


class Ctx:
    def __init__(self):
        self.queues = []
        self.streams = []

    def _wait(self, q, tok, waits):
        oq, si, val = tok
        if oq is q and q.pe:
            return
        key = (oq.name, si)
        seen = q.stream.seen
        if seen.get(key, 0) >= val:
            return
        seen[key] = val
        waits.append((oq.sems[si], val))

    def op(self, q, fn, reads=(), writes=(), inc=True):
        waits = []
        for t in reads:
            if t.w is not None:
                self._wait(q, t.w, waits)
        for t in writes:
            if t.w is not None:
                self._wait(q, t.w, waits)
            for tok in t.r.values():
                self._wait(q, tok, waits)
        if q.dma:
            K = len(q.sems)
            i = q.n
            si = i % K
            val = 16 * (i // K + 1)
            if i >= K:
                self._wait(q, (q, si, val - 16), waits)
            tok = (q, si, val)
            q.n += 1
            q.stream.prog.append((waits, fn, (q.sems[si], 16)))
        else:
            tok = (q, 0, q.n + 1)
            if inc:
                q.n += 1
                q.stream.prog.append((waits, fn, (q.sems[0], 1)))
            else:
                q.stream.prog.append((waits, fn, None))
        for t in reads:
            t.r[(q.name, tok[1])] = tok
        for t in writes:
            t.w = tok
            t.r = {}
        return tok

    def last_tokens(self, q):
        if q.dma:
            K = len(q.sems)
            out = []
            for si in range(K):
                cnt = (q.n - si + K - 1) // K if q.n > si else 0
                if cnt > 0:
                    out.append((q, si, 16 * cnt))
            return out
        return [(q, 0, q.n)] if q.n > 0 else []

    def barrier(self):
        for st in self.streams:
            waits = []
            for oq in self.queues:
                for tok in self.last_tokens(oq):
                    key = (oq.name, tok[1])
                    if st.seen.get(key, 0) >= tok[2]:
                        continue
                    st.seen[key] = tok[2]
                    waits.append((oq.sems[tok[1]], tok[2]))
            if waits:
                st.prog.append((waits, None, None))


def replay(stream, eng):
    for waits, fn, inc in stream.prog:
        for sem, val in waits:
            eng.wait_ge(sem, val)
        if fn is None:
            continue
        ins = fn(eng)
        if inc is not None:
            ins.then_inc(inc[0], inc[1])
    stream.prog = []


def build_nc(stop_after=99, debug=False):
    nc = bass.Bass("TRN2", target_bir_lowering=False)
    import contextlib
    es = contextlib.ExitStack()

    def din(name, shape, dt=F32):
        return nc.dram_tensor(name, list(shape), dt, kind="ExternalInput").ap()

    def dscr(name, shape, dt):
        return nc.dram_tensor(name, list(shape), dt, kind=("ExternalOutput" if debug else "Internal")).ap()

    x_own = din("x_own", [NOWN, D])
    x_oth = din("x_oth", [3 * NOWN, D])
    rope_own = din("rope_own", [NOWN, 128])
    rope_oth = din("rope_oth", [3 * NOWN, 128])
    p_own = din("p_own", [NOWN, PLE])
    w_in = din("w_in", [D, DIN])
    w_fsel = din("w_fsel", [3 * D, D])
    lbsel = din("lbsel", [128, 3, 2, 16])
    lbown = din("lbown", [128, 2, 2, 16])
    flags = din("flags", [128, 8])
    g_mix = din("g_mix", [D])
    g_q = din("g_q", [HD])
    g_k = din("g_k", [HD])
    g_hgrn = din("g_hgrn", [HD])
    g_mlp = din("g_mlp", [D])
    g_ple = din("g_ple", [D])
    g_final = din("g_final", [D])
    w_o_attn = din("w_o_attn", [D, D])
    w_o_hgrn = din("w_o_hgrn", [D, D])
    w_out = din("w_out", [D, D])
    w_up = din("w_up", [D, DFF])
    w_down = din("w_down", [DFF, D])
    w_ple_gate = din("w_ple_gate", [D, D])
    w_ple = din("w_ple", [PLE, D])
    c_ident = din("c_ident", [128, 128], BF16)
    c_maskF = din("c_maskF", [128, 128])
    c_maskB = din("c_maskB", [128, 128])
    c_resetF = din("c_resetF", [128, 512])
    c_resetB = din("c_resetB", [128, 512])
    out = nc.dram_tensor("out", [NOWN, D], F32, kind="ExternalOutput").ap()

    KT_d = dscr("KT_d", [NKVH, 128, S], BF16)
    V_d = dscr("V_d", [S, NKVH * HD], BF16)
    QT_d = dscr("QT_d", [NQH, 128, NOWN], BF16)
    hT_d = dscr("hT_d", [128, 16, NOWN], BF16)
    sq_d = dscr("sq_d", [NQH, 128, NOWN], BF16)
    v_d = dscr("v_d", [NOWN, D], BF16)
    g_d = dscr("g_d", [NOWN, D], BF16)
    ofw_d = dscr("ofw_d", [NQH, NOWN, 128], F32)
    gA_d = dscr("gA_d", [D, NOWN], BF16)
    gR_d = dscr("gR_d", [D, NOWN], BF16)
    orT_d = dscr("orT_d", [D, NOWN], BF16)
    oaT_d = dscr("oaT_d", [D, NOWN], BF16)
    Sdbg_d = dscr("Sdbg_d", [2, 128, 16 * 128], F32)
    WSRC = {"w_in": (w_in, [D, DIN]), "w_fsel": (w_fsel, [3 * D, D]), "w_o_attn": (w_o_attn, [D, D]),
            "w_o_hgrn": (w_o_hgrn, [D, D]), "w_out": (w_out, [D, D]), "w_up": (w_up, [D, DFF]),
            "w_down": (w_down, [DFF, D]), "w_ple_gate": (w_ple_gate, [D, D]), "w_ple": (w_ple, [PLE, D])}
    WBF = {n: nc.dram_tensor("bf_" + n, shp, BF16, kind="Internal").ap() for n, (_, shp) in WSRC.items()}
    WTK = {}

    cx = Ctx()

    def mkq(name, nsem, stream, dma=False, pe=False):
        sems = [es.enter_context(nc.semaphore(f"{name}_s{i}")) for i in range(nsem)]
        q = Qu(name, sems, stream, dma=dma, pe=pe)
        cx.queues.append(q)
        return q

    st_pe, st_act, st_dve, st_pool, st_sp = (Stream(n) for n in ("pe", "act", "dve", "pool", "sp"))
    cx.streams = [st_pe, st_act, st_dve, st_pool, st_sp]
    PE = mkq("pe", 1, st_pe, pe=True)
    ACT = mkq("act", 1, st_act)
    DVE = mkq("dve", 1, st_dve)
    POOL = mkq("pool", 1, st_pool)
    SPD = mkq("spd", 8, st_sp, dma=True)
    GPD = mkq("gpd", 8, st_pool, dma=True)

    def run_block():
        with nc.Block() as block:
            @block.tensor
            def _(e):
                replay(st_pe, e)

            @block.scalar
            def _(e):
                replay(st_act, e)

            @block.vector
            def _(e):
                replay(st_dve, e)

            @block.gpsimd
            def _(e):
                replay(st_pool, e)

            @block.sync
            def _(e):
                replay(st_sp, e)

    def sb(name, shape, dt):
        return es2.enter_context(nc.sbuf_tensor(name, list(shape), dt))

    PB = [es.enter_context(nc.psum_tensor(f"pb{i}", [128, 512], F32)) for i in range(6)]
    PBT = [es.enter_context(nc.psum_tensor(f"pbt{i}", [128, 1024], BF16)) for i in range(2)]
    PBk = [Tk() for _ in range(6)]
    PBTk = [Tk() for _ in range(2)]

    def psb(name, shape, dt):
        return es.enter_context(nc.sbuf_tensor(name, list(shape), dt))

    ident = psb("ident", [128, 128], BF16)
    maskF = psb("maskF", [128, 128], F32)
    maskB = psb("maskB", [128, 128], F32)
    resetF = psb("resetF", [128, 512], F32)
    resetB = psb("resetB", [128, 512], F32)
    flg = psb("flg", [128, 8], F32)
    gq_b = psb("gq_b", [128, HD], F32)
    gk_b = psb("gk_b", [128, HD], F32)
    gh_b = psb("gh_b", [128, HD], F32)
    Sst = psb("Sst", [128, 16, 128], F32)
    Sfw = psb("Sfw", [128, 16, 128], F32)
    Sbf = psb("Sbf", [128, 16, 128], BF16)
    lb_own = psb("lb_own", [128, 2, 16], F32)
    oml_own = psb("oml_own", [128, 2, 16], F32)
    nml_own = psb("nml_own", [128, 2, 16], F32)
    lb_sel = psb("lb_sel", [128, 3, 16], F32)
    oml_sel = psb("oml_sel", [128, 3, 16], F32)
    nml_sel = psb("nml_sel", [128, 3, 16], F32)
    k_const = Tk()
    k_S = [Tk() for _ in range(16)]
    k_Sfw = Tk()
    k_out = Tk()
    k_Sbf = [Tk() for _ in range(16)]

    es2 = es
    if True:
        lbr = psb("lbr", [128, 2, 2, 16], F32)
        lbs = psb("lbs", [128, 3, 2, 16], F32)
        tmpc = psb("tmpc", [128, 5, 16], F32)
        k_tmp = Tk()
        for dst, src in ((ident, c_ident), (maskF, c_maskF), (maskB, c_maskB), (resetF, c_resetF),
                         (resetB, c_resetB), (flg, flags)):
            cx.op(SPD, lambda e, d=dst, s=src: e.dma_start(out=d[:], in_=s), writes=[k_const])
        for dst, src in ((gq_b, g_q), (gk_b, g_k), (gh_b, g_hgrn)):
            cx.op(SPD, lambda e, d=dst, s=src: e.dma_start(out=d[:], in_=s.partition_broadcast(128)),
                  writes=[k_const])
        cx.op(SPD, lambda e: e.dma_start(out=lbr[:], in_=lbown), writes=[k_tmp])
        cx.op(SPD, lambda e: e.dma_start(out=lbs[:], in_=lbsel), writes=[k_tmp])

        def lbcalc(raw, n, lb_t, oml_t, nml_t):
            d_ = tmpc[:, 0:n, :]
            cx.op(DVE, lambda e: e.tensor_tensor(out=d_, in0=raw[:, :, 0, :], in1=raw[:, :, 1, :], op=ALU.subtract),
                  reads=[k_tmp], writes=[k_tmp])
            cx.op(ACT, lambda e: e.activation(out=lb_t[:], in_=d_, func=AF.Sigmoid), reads=[k_tmp], writes=[k_const])
            cx.op(DVE, lambda e: e.tensor_scalar(out=oml_t[:], in0=lb_t[:], scalar1=-1.0, scalar2=1.0,
                                                 op0=ALU.mult, op1=ALU.add), reads=[k_const], writes=[k_const])
            cx.op(DVE, lambda e: e.tensor_scalar(out=nml_t[:], in0=lb_t[:], scalar1=1.0, scalar2=-1.0,
                                                 op0=ALU.mult, op1=ALU.add), reads=[k_const], writes=[k_const])

        lbcalc(lbr, 2, lb_own, oml_own, nml_own)
        lbcalc(lbs, 3, lb_sel, oml_sel, nml_sel)
        cx.op(DVE, lambda e: e.memset(Sst[:], 0.0), writes=k_S)
        cx.op(DVE, lambda e: e.memset(Sfw[:], 0.0), writes=[k_Sfw])
        cx.op(POOL, lambda e: e.memset(Sbf[:], 0.0), writes=k_Sbf)

        def conv(name, r0, nr, c0, ncols):
            src, _ = WSRC[name]
            tk = Tk()
            WTK[(name, r0, c0)] = (tk, nr, ncols)
            for cc in range(c0, c0 + ncols, 2048):
                w_ = min(2048, c0 + ncols - cc)
                cx.op(GPD, lambda e, cc=cc, w_=w_: e.dma_start(out=WBF[name][r0:r0 + nr, cc:cc + w_],
                                                               in_=src[r0:r0 + nr, cc:cc + w_]), [], [tk])
        conv("w_in", 0, D, C_K, 1024)
        conv("w_in", 0, D, C_I, 2048)
        for g in range(3):
            conv("w_fsel", g * D, D, 0, 2048)
        later = [("w_in", 0, D, c0, 2048) for c0 in (C_Q, C_G, C_GA, C_GR, C_QR, C_FF, C_FB)]
        later += [("w_o_attn", 0, D, 0, D), ("w_o_hgrn", 0, D, 0, D), ("w_out", 0, D, 0, D)]
        later += [("w_up", 0, D, q_ * 2048, 2048) for q_ in range(4)]
        later += [("w_down", q_ * 2048, 2048, 0, 2048) for q_ in range(4)]
        later += [("w_ple_gate", 0, D, 0, D), ("w_ple", 0, PLE, 0, D)]

        def issue_convs(n):
            for _ in range(n):
                if later:
                    conv(*later.pop(0))
        run_block()

    state = {"pb": 0, "w": 0, "pt": 0, "ev": 0}

    def act(out, in_, func, r, w, **kw):
        return cx.op(ACT, lambda e: e.activation(out=out, in_=in_, func=func, **kw), r, w)

    def tt(q, out, in0, in1, op, r, w):
        return cx.op(q, lambda e: e.tensor_tensor(out=out, in0=in0, in1=in1, op=op), r, w)

    def ts(q, out, in0, s1, s2, op0, op1, r, w):
        return cx.op(q, lambda e: e.tensor_scalar(out=out, in0=in0, scalar1=s1, scalar2=s2, op0=op0, op1=op1), r, w)

    def stt(out, in0, scalar, in1, op0, op1, r, w):
        return cx.op(DVE, lambda e: e.scalar_tensor_tensor(out=out, in0=in0, scalar=scalar, in1=in1, op0=op0, op1=op1), r, w)

    def recip(out, in_, r, w):
        return cx.op(DVE, lambda e: e.reciprocal(out=out, in_=in_), r, w)

    def dma(q, out, in_, r, w):
        return cx.op(q, lambda e: e.dma_start(out=out, in_=in_), r, w)

    def mm(out, lhsT, rhs, start, stop, r, w, inc):
        return cx.op(PE, lambda e: e.matmul(out, lhsT, rhs, start=start, stop=stop), r, w, inc=inc)

    def trn(out, in_, r, w, inc):
        return cx.op(PE, lambda e: e.transpose(out, in_, ident[:]), list(r) + [k_const], w, inc=inc)

    def evac(dst, src, r, w, eng=None):
        if eng is None:
            eng = ACT if state["ev"] % 2 == 0 else DVE
            state["ev"] += 1
        if eng is ACT:
            return cx.op(ACT, lambda e: e.copy(out=dst, in_=src), r, w)
        return cx.op(eng, lambda e: e.tensor_copy(out=dst, in_=src), r, w)

    def next_pb():
        i = state["pb"] % 4
        state["pb"] += 1
        return PB[i], PBk[i]

    def next_pt():
        i = state["pt"] % 2
        state["pt"] += 1
        return PBT[i][:, :].rearrange("p (a b) -> p a b", a=8), PBTk[i]

    def load_w(wbufs, wks, name, r0, nkc, c0, ncols=512):
        i = state["w"] % len(wbufs)
        state["w"] += 1
        wb, wk = wbufs[i], wks[i]
        tk = None
        for (n_, rr, cc), (t_, nr, ncl) in WTK.items():
            if n_ == name and rr <= r0 and r0 + nkc * 128 <= rr + nr and cc <= c0 and c0 + ncols <= cc + ncl:
                tk = t_
        assert tk is not None, (name, r0, c0)
        src = WBF[name][r0:r0 + nkc * 128, c0:c0 + ncols].rearrange("(kc p) n -> p kc n", p=128)
        dma(SPD, wb[:, 0:nkc, 0:ncols], src, [tk], [wk])
        return wb, wk

    def mm_group(out_ap, out_k, pairs, reads):
        n = len(pairs)
        for i, (l, r_) in enumerate(pairs):
            mm(out_ap, l, r_, i == 0, i == n - 1, reads, [out_k], i == n - 1)

    class NS:
        pass

    def norm_tile(x_ap, k_x, gbt, hT, k_hT, ti, W, scale_d=D):
        act(W.hb[:], x_ap, AF.Square, [k_x], [W.k_hb, W.k_stat], accum_out=W.stat[:, 0:1])
        act(W.stat[:, 1:2], W.stat[:, 0:1], AF.Ln, [W.k_stat, k_const], [W.k_stat], scale=1.0 / scale_d, bias=epst[:, 0:1])
        act(W.stat[:, 2:3], W.stat[:, 1:2], AF.Exp, [W.k_stat], [W.k_stat], scale=-0.5)
        stt(W.hb[:], x_ap, W.stat[:, 2:3], gbt[:], ALU.mult, ALU.mult, [k_x, W.k_stat, k_const], [W.k_hb])
        for half in range(2):
            pt, pk = next_pt()
            for j in range(8):
                kc = half * 8 + j
                trn(pt[:, j, :], W.hb[:, kc * 128:(kc + 1) * 128], [W.k_hb], [pk], j == 7)
            evac(hT[:, half * 8:(half + 1) * 8, ti * 128:(ti + 1) * 128], pt, [pk], [k_hT])

    def headnorm_rope(ps_ap, k_ps, nh, gb, ropet, k_rope, dst, k_dst, W):
        n = nh * 128
        sq = W.sq[:, 0:n]
        act(sq, ps_ap, AF.Square, [k_ps], [W.k_sq])
        cx.op(DVE, lambda e: e.tensor_reduce(out=W.hs[:, 0:nh], in_=sq.rearrange("p (h d) -> p h d", h=nh),
                                             axis=AX.X, op=ALU.add), [W.k_sq], [W.k_hs])
        act(W.hs[:, 16:16 + nh], W.hs[:, 0:nh], AF.Ln, [W.k_hs, k_const], [W.k_hs], scale=1.0 / HD, bias=epst[:, 0:1])
        act(W.hs[:, 32:32 + nh], W.hs[:, 16:16 + nh], AF.Exp, [W.k_hs], [W.k_hs], scale=-0.5)
        kn = W.kn[:, 0:n]
        kn3 = kn.rearrange("p (h d) -> p h d", h=nh)
        tt(DVE, kn3, ps_ap.rearrange("p (h d) -> p h d", h=nh),
           W.hs[:, 32:32 + nh].unsqueeze(2).broadcast_to([128, nh, 128]), ALU.mult, [k_ps, W.k_hs], [W.k_kn])
        tt(POOL, kn3, kn3, gb[:].unsqueeze(1).broadcast_to([128, nh, 128]), ALU.mult, [W.k_kn, k_const], [W.k_kn])
        x5 = kn.rearrange("p (h a b c) -> p h a b c", h=nh, a=2, b=2)
        d5 = dst.rearrange("p (h a b c) -> p h a b c", h=nh, a=2, b=2)
        x1, x2 = x5[:, :, :, 0, :], x5[:, :, :, 1, :]
        Cc = ropet[:, 0:64].rearrange("p (a c) -> p a c", a=2).unsqueeze(1).broadcast_to([128, nh, 2, 32])
        Sn = ropet[:, 64:128].rearrange("p (a c) -> p a c", a=2).unsqueeze(1).broadcast_to([128, nh, 2, 32])
        m = nh * 64
        t1 = W.t1[:, 0:m].rearrange("p (h a c) -> p h a c", h=nh, a=2)
        t2 = W.t2[:, 0:m].rearrange("p (h a c) -> p h a c", h=nh, a=2)
        t3 = W.t3[:, 0:m].rearrange("p (h a c) -> p h a c", h=nh, a=2)
        t4 = W.t4[:, 0:m].rearrange("p (h a c) -> p h a c", h=nh, a=2)
        tt(POOL, t1, x1, Cc, ALU.mult, [W.k_kn, k_rope], [W.k_t])
        tt(POOL, t2, x2, Sn, ALU.mult, [W.k_kn, k_rope], [W.k_t])
        tt(POOL, t3, x1, Sn, ALU.mult, [W.k_kn, k_rope], [W.k_t])
        tt(POOL, t4, x2, Cc, ALU.mult, [W.k_kn, k_rope], [W.k_t])
        tt(POOL, d5[:, :, :, 0, :], t1, t2, ALU.subtract, [W.k_t], [k_dst])
        tt(POOL, d5[:, :, :, 1, :], t3, t4, ALU.add, [W.k_t], [k_dst])

    es2 = contextlib.ExitStack()
    with es2:
        W = NS()
        W.hb = sb("hb", [128, D], BF16); W.k_hb = Tk()
        W.stat = sb("stat", [128, 4], F32); W.k_stat = Tk()
        W.sq = sb("wsq", [128, 512], F32); W.k_sq = Tk()
        W.hs = sb("whs", [128, 48], F32); W.k_hs = Tk()
        W.kn = sb("wkn", [128, 512], F32); W.k_kn = Tk()
        W.t1 = sb("wt1", [128, 256], F32); W.t2 = sb("wt2", [128, 256], F32)
        W.t3 = sb("wt3", [128, 256], F32); W.t4 = sb("wt4", [128, 256], F32); W.k_t = Tk()
        epst = sb("epst", [128, 1], F32)
        gmix_b = sb("gmix_b", [128, D], F32)
        xt = [sb(f"xt{i}", [128, D], F32) for i in range(2)]; k_xt = [Tk(), Tk()]
        hTs = [sb(f"hT{i}", [128, 16, 512], BF16) for i in range(2)]; k_hTs = [Tk(), Tk()]
        cur = {"i": 0}
        wbufs = [sb(f"wb{i}", [128, 16, 512], BF16) for i in range(2)]; wks = [Tk(), Tk()]
        ropeb = sb("ropeb", [128, 4, 128], F32); k_rope = Tk()
        qk_bf = sb("qk_bf", [128, 512], BF16); k_qkbf = Tk()
        KTst = sb("KTst", [128, 4, 512], BF16); k_KTst = Tk()
        Vst = sb("Vst", [128, 4, 512], BF16); k_Vst = Tk()
        QTst = sb("QTst", [128, 16, 512], BF16); k_QTst = Tk()
        vblk = sb("vblk", [128, 4, 2048], BF16); k_vblk = Tk()
        sig = sb("sig", [128, 512], F32); k_sig = Tk()
        logf = sb("logf", [128, 512], F32); k_logf = Tk()
        kTt = sb("kTt", [128, 512], F32); k_kT = Tk()
        Asc = sb("Asc", [128, 512], F32); k_A = Tk()
        eA = sb("eA", [128, 512], F32); k_eA = Tk()
        enA = sb("enA", [128, 512], F32); k_enA = Tk()
        ones512 = sb("ones512", [128, 512], F32)
        wT = sb("wT", [128, 512], BF16); k_wT = Tk()
        wtm = sb("wtm", [128, 4, 128], BF16); k_wtm = Tk()
        sqT = sb("sqT", [128, 512], BF16); k_sqT = Tk()
        qlo = sb("qlo", [128, 4, 128], BF16); qhi = sb("qhi", [128, 4, 128], BF16); k_ql = Tk()
        scm = sb("scm", [128, 128], BF16); k_scm = Tk()
        Ut = sb("Ut", [128, 128], F32); k_U = Tk()
        Sbf2 = sb("Sbf2", [128, 128], BF16); k_Sbf2 = Tk()
        ebt = sb("ebt", [128, 2], F32); k_eb = Tk()
        ost = sb("ost", [128, 4, 128], F32); k_ost = Tk()
        ofwst = sb("ofwst", [128, 4, 128], F32); k_ofwst = Tk()
        gst = sb("gst", [128, 4, 128], BF16); k_gst = Tk()
        otot = sb("otot", [128, 128], F32); k_otot = Tk()
        orb = sb("orb", [128, 128], BF16); k_orb = Tk()
        orTst = sb("orTst", [128, 512], BF16); k_orTst = Tk()
        gevs = [sb(f"gev{i}", [128, 512], BF16) for i in range(2)]; k_gevs = [Tk(), Tk()]
        gevi = {"i": 0}

        def next_gev():
            i = gevi["i"] % 2
            gevi["i"] += 1
            return gevs[i], k_gevs[i]

        cx.op(DVE, lambda e: e.memset(epst[:], EPS), [], [k_const])
        cx.op(DVE, lambda e: e.memset(ones512[:], 1.0), [], [k_const])
        cx.op(DVE, lambda e: e.memset(Asc[:], 0.0), [], [k_A])
        cx.op(POOL, lambda e: e.memset(qlo[:], 0.0), [], [k_ql])
        cx.op(POOL, lambda e: e.memset(qhi[:], 0.0), [], [k_ql])
        dma(SPD, gmix_b[:], g_mix.partition_broadcast(128), [], [k_const])

        def make_hT(xsrc, t0, slot, store_hT=None):
            hT, k_hT = hTs[slot], k_hTs[slot]
            for ti in range(4):
                x_t, kx = xt[ti % 2], k_xt[ti % 2]
                dma(SPD, x_t[:], xsrc[t0 + ti * 128:t0 + (ti + 1) * 128, :], [], [kx])
                norm_tile(x_t[:], kx, gmix_b, hT, k_hT, ti, W)
            if store_hT is not None:
                dma(GPD, hT_d[:, :, store_hT:store_hT + 512], hT[:], [k_hT], [k_hTd])

        def proj_tm(src2d, r0, c0, consumer):
            wb, wk = load_w(wbufs, wks, src2d, r0, 16, c0)
            hT, k_hT = hTs[cur["i"]], k_hTs[cur["i"]]
            for ti in range(4):
                ps, kp = next_pb()
                mm_group(ps[:], kp, [(hT[:, kc, ti * 128:(ti + 1) * 128], wb[:, kc, :]) for kc in range(16)], [k_hT, wk])
                consumer(ti, ps[:], kp)

        def proj_fm(src2d, r0, c0, consumer):
            wb, wk = load_w(wbufs, wks, src2d, r0, 16, c0)
            hT, k_hT = hTs[cur["i"]], k_hTs[cur["i"]]
            for j in range(4):
                ps, kp = next_pb()
                mm_group(ps[:], kp, [(wb[:, kc, j * 128:(j + 1) * 128], hT[:, kc, :]) for kc in range(16)], [k_hT, wk])
                consumer(j, ps[:], kp)

        def kv_block(ropesrc, t0, pos):
            dma(SPD, ropeb[:], ropesrc[t0:t0 + 512, :].rearrange("(ti p) c -> p ti c", p=128), [], [k_rope])

            def kcons(ti, ps, kp):
                headnorm_rope(ps, kp, 4, gk_b, ropeb[:, ti, :], k_rope, qk_bf[:, 0:512], k_qkbf, W)
                pt, pk = next_pt()
                for h in range(4):
                    trn(pt[:, h, :], qk_bf[:, h * 128:(h + 1) * 128], [k_qkbf], [pk], h == 3)
                evac(KTst[:, :, ti * 128:(ti + 1) * 128], pt[:, 0:4, :], [pk], [k_KTst])
            proj_tm("w_in", 0, C_K, kcons)
            dma(GPD, KT_d[:, :, pos:pos + 512].rearrange("h p t -> p h t"), KTst[:], [k_KTst], [k_KTd])

            def vcons(ti, ps, kp):
                evac(Vst[:, ti, :], ps, [kp], [k_Vst])
            proj_tm("w_in", 0, C_V, vcons)
            dma(GPD, V_d[pos:pos + 512, :].rearrange("(ti p) c -> p ti c", p=128), Vst[:], [k_Vst], [k_Vd])

        def i_block(store_t0=None):
            for cg in range(4):
                def icons(ti, ps, kp, cg=cg):
                    evac(vblk[:, ti, cg * 512:(cg + 1) * 512], ps, [kp], [k_vblk])
                proj_tm("w_in", 0, C_I + cg * 512, icons)
            if store_t0 is not None:
                dma(GPD, v_d[store_t0:store_t0 + 512, :].rearrange("(ti p) c -> p ti c", p=128), vblk[:], [k_vblk], [k_vd])

        def gates_common(ps, kp, lbt, omlt, nmlt, idx, h):
            onec = ones512[:, 0:1]
            act(sig[:], ps, AF.Exp, [kp], [k_sig], scale=-1.0)
            act(logf[:], sig[:], AF.Ln, [k_sig, k_const], [k_logf], scale=1.0, bias=onec)
            act(kTt[:], sig[:], AF.Ln, [k_sig, k_const], [k_kT], scale=lbt[:, idx, h:h + 1], bias=onec)
            act(sig[:], logf[:], AF.Exp, [k_logf], [k_sig], scale=-1.0)
            tt(DVE, logf[:], kTt[:], logf[:], ALU.subtract, [k_kT, k_logf], [k_logf])
            ts(DVE, kTt[:], sig[:], nmlt[:, idx, h:h + 1], omlt[:, idx, h:h + 1], ALU.mult, ALU.add, [k_sig, k_const], [k_kT])

        def state_only_head(ps, kp, g, h):
            gates_common(ps, kp, lb_sel, oml_sel, nml_sel, g, h)
            cx.op(DVE, lambda e: e.tensor_tensor_scan(out=Asc[:, 510::-1], data0=ones512[:, 0:511], data1=logf[:, 511:0:-1],
                                                      initial=0.0, op0=ALU.mult, op1=ALU.add), [k_logf, k_const], [k_A])
            act(eA[:], Asc[:], AF.Exp, [k_A], [k_eA])
            tt(POOL, wT[:], kTt[:], eA[:], ALU.mult, [k_kT, k_eA], [k_wT])
            ts(DVE, ebt[:, 0:1], kTt[:, 0:1], -1.0, 1.0, ALU.mult, ALU.add, [k_kT], [k_eb])
            tt(DVE, ebt[:, 1:2], ebt[:, 0:1], eA[:, 0:1], ALU.mult, [k_eb, k_eA], [k_eb])
            pt, pk = next_pt()
            for ti in range(4):
                trn(pt[:, ti, :], wT[:, ti * 128:(ti + 1) * 128], [k_wT], [pk], ti == 3)
            evac(wtm[:], pt[:, 0:4, :], [pk], [k_wtm])
            mm_group(PB[5][:, 0:128], PBk[5], [(wtm[:, ti, :], vblk[:, ti, h * 128:(h + 1) * 128]) for ti in range(4)],
                     [k_wtm, k_vblk])
            stt(Sst[:, h, :], Sst[:, h, :], ebt[:, 1:2], PB[5][:, 0:128], ALU.mult, ALU.add, [k_S[h], k_eb, PBk[5]], [k_S[h]])

        def own_head(ps, kp, d, h, t0, final):
            Srun = Sfw if d == 0 else Sst
            kS = k_Sfw if d == 0 else k_S[h]
            gates_common(ps, kp, lb_own, oml_own, nml_own, d, h)
            if d == 0:
                cx.op(DVE, lambda e: e.tensor_tensor_scan(out=Asc[:], data0=resetF[:], data1=logf[:], initial=0.0,
                                                          op0=ALU.mult, op1=ALU.add), [k_logf, k_const], [k_A])
            else:
                cx.op(DVE, lambda e: e.tensor_tensor_scan(out=Asc[:, ::-1], data0=resetB[:, ::-1], data1=logf[:, ::-1],
                                                          initial=0.0, op0=ALU.mult, op1=ALU.add), [k_logf, k_const], [k_A])
            act(eA[:], Asc[:], AF.Exp, [k_A], [k_eA])
            act(enA[:], Asc[:], AF.Exp, [k_A], [k_enA], scale=-1.0)
            sq3 = sqT[:].rearrange("p (t c) -> p t c", t=4)
            eA3 = eA[:].rearrange("p (t c) -> p t c", t=4)
            tt(POOL, qlo[:, :, 0:64], sq3[:, :, 0:64], eA3[:, :, 0:64], ALU.mult, [k_sqT, k_eA], [k_ql])
            tt(POOL, qhi[:, :, 64:128], sq3[:, :, 64:128], eA3[:, :, 64:128], ALU.mult, [k_sqT, k_eA], [k_ql])
            tt(DVE, wT[:], kTt[:], enA[:], ALU.mult, [k_kT, k_enA], [k_wT])
            pt, pk = next_pt()
            for ti in range(4):
                trn(pt[:, ti, :], wT[:, ti * 128:(ti + 1) * 128], [k_wT], [pk], ti == 3)
            evac(wtm[:], pt[:, 0:4, :], [pk], [k_wtm])
            if final:
                dma(SPD, ofwst[:], ofw_d[h, t0:t0 + 512, :].rearrange("(ti p) v -> p ti v", p=128), [k_ofwd], [k_ofwst])
                dma(SPD, gst[:], g_d[t0:t0 + 512, h * 128:(h + 1) * 128].rearrange("(ti p) v -> p ti v", p=128), [k_gd], [k_gst])
            mask = maskF if d == 0 else maskB
            tiles = range(4) if d == 0 else range(3, -1, -1)
            for ti in tiles:
                sc = PB[4][:, 0:128]
                mm(sc, wT[:, ti * 128:(ti + 1) * 128], qlo[:, ti, :], True, False, [k_wT, k_ql], [PBk[4]], False)
                mm(sc, wT[:, ti * 128:(ti + 1) * 128], qhi[:, ti, :], False, True, [k_wT, k_ql], [PBk[4]], True)
                tt(DVE, scm[:], sc, mask[:], ALU.mult, [PBk[4], k_const], [k_scm])
                c1, c2 = (0, 64) if d == 0 else (64, 0)
                e1 = eA[:, ti * 128 + (63 if d == 0 else 64): ti * 128 + (63 if d == 0 else 64) + 1]
                e2 = eA[:, ti * 128 + (127 if d == 0 else 0): ti * 128 + (127 if d == 0 else 0) + 1]
                q1, q2 = (qlo, qhi) if d == 0 else (qhi, qlo)
                dS = PB[5][:, 0:128]
                mm(dS, wtm[c1:c1 + 64, ti, :], vblk[c1:c1 + 64, ti, h * 128:(h + 1) * 128], True, True,
                   [k_wtm, k_vblk], [PBk[5]], True)
                tt(DVE, Ut[:], Srun[:, h, :], dS, ALU.add, [kS, PBk[5]], [k_U])
                ts(POOL, Srun[:, h, :], Ut[:], e1, None, ALU.mult, ALU.bypass, [k_U, k_eA], [kS])
                act(Sbf2[:], Ut[:], AF.Copy, [k_U, k_eA], [k_Sbf2], scale=e1)
                op_ = PB[5][:, 128:256]
                mm(op_, q1[:, ti, :], Sbf[:, h, :], True, False, [k_ql, k_Sbf[h]], [PBk[6]], False)
                mm(op_, q2[:, ti, :], Sbf2[:], False, False, [k_ql, k_Sbf2], [PBk[6]], False)
                mm(op_, scm[:], vblk[:, ti, h * 128:(h + 1) * 128], False, True, [k_scm, k_vblk], [PBk[6]], True)
                dS2 = PB[5][:, 256:384]
                mm(dS2, wtm[c2:c2 + 64, ti, :], vblk[c2:c2 + 64, ti, h * 128:(h + 1) * 128], True, True,
                   [k_wtm, k_vblk], [PBk[7]], True)
                tt(DVE, Ut[:], Srun[:, h, :], dS2, ALU.add, [kS, PBk[7]], [k_U])
                ts(POOL, Srun[:, h, :], Ut[:], e2, None, ALU.mult, ALU.bypass, [k_U, k_eA], [kS])
                act(Sbf[:, h, :], Ut[:], AF.Copy, [k_U, k_eA], [k_Sbf[h]], scale=e2)
                if not final:
                    evac(ost[:, ti, :], op_, [PBk[6]], [k_ost], eng=ACT)
                else:
                    tt(DVE, otot[:], op_, ofwst[:, ti, :], ALU.add, [PBk[6], k_ofwst], [k_otot])
                    act(orb[:], otot[:], AF.Square, [k_otot], [k_orb, W.k_stat], accum_out=W.stat[:, 0:1])
                    act(W.stat[:, 1:2], W.stat[:, 0:1], AF.Sqrt, [W.k_stat, k_const], [W.k_stat], scale=1.0 / HD, bias=epst[:, 0:1])
                    recip(W.stat[:, 2:3], W.stat[:, 1:2], [W.k_stat], [W.k_stat])
                    stt(otot[:], otot[:], W.stat[:, 2:3], gh_b[:], ALU.mult, ALU.mult, [k_otot, W.k_stat, k_const], [k_otot])
                    tt(DVE, orb[:], otot[:], gst[:, ti, :], ALU.mult, [k_otot, k_gst], [k_orb])
                    pt2, pk2 = next_pt()
                    trn(pt2[:, 0, :], orb[:], [k_orb], [pk2], True)
                    evac(orTst[:, ti * 128:(ti + 1) * 128], pt2[:, 0, :], [pk2], [k_orTst])
            if not final:
                dma(GPD, ofw_d[h, t0:t0 + 512, :].rearrange("(ti p) v -> p ti v", p=128), ost[:], [k_ost], [k_ofwd])
            else:
                dma(GPD, orT_d[h * 128:(h + 1) * 128, t0:t0 + 512], orTst[:], [k_orTst], [k_orTd])

        k_KTd, k_Vd, k_QTd, k_hTd, k_sqd, k_vd, k_gd, k_ofwd, k_gAd, k_gRd, k_orTd, k_oaTd = (Tk() for _ in range(12))
        PBk.append(Tk()); PBk.append(Tk())

        if stop_after >= 1:
            make_hT(x_oth, 0, 0)
            cur["i"] = 0
            for g in range(3):
                ts(DVE, Sst[:].rearrange("p h v -> p (h v)"), Sst[:].rearrange("p h v -> p (h v)"), flg[:, g:g + 1], None,
                   ALU.mult, ALU.bypass, k_S + [k_const], k_S)
                for blk in range(4):
                    t0 = g * NOWN + blk * 512
                    kv_block(rope_oth, t0, NOWN + t0)
                    issue_convs(2)
                    if t0 + 512 < 3 * NOWN:
                        make_hT(x_oth, t0 + 512, 1 - cur["i"])
                    elif stop_after >= 2:
                        make_hT(x_own, 0, 1 - cur["i"], store_hT=0)
                    i_block()
                    for cg in range(4):
                        def fcons(j, ps, kp, cg=cg, g=g):
                            state_only_head(ps, kp, g, cg * 4 + j)
                        proj_fm("w_fsel", g * D, cg * 512, fcons)
                    cur["i"] = 1 - cur["i"]
                stt(Sfw[:].rearrange("p h v -> p (h v)"), Sst[:].rearrange("p h v -> p (h v)"), flg[:, 3 + g:4 + g],
                    Sfw[:].rearrange("p h v -> p (h v)"), ALU.mult, ALU.add, k_S + [k_Sfw, k_const], [k_Sfw])
            ts(DVE, Sst[:].rearrange("p h v -> p (h v)"), Sst[:].rearrange("p h v -> p (h v)"), flg[:, 6:7], None,
               ALU.mult, ALU.bypass, k_S + [k_const], k_S)
            if debug:
                dma(GPD, Sdbg_d[0], Sfw[:].rearrange("p h v -> p (h v)"), [k_Sfw], [Tk()])
                dma(GPD, Sdbg_d[1], Sst[:].rearrange("p h v -> p (h v)"), k_S, [Tk()])

        issue_convs(100)
        if stop_after >= 2:
            for h in range(16):
                evac(Sbf[:, h, :], Sfw[:, h, :], [k_Sfw], [k_Sbf[h]])
            if stop_after < 1:
                make_hT(x_own, 0, cur["i"], store_hT=0)
            for blk in range(4):
                t0 = blk * 512
                hT, k_hT = hTs[cur["i"]], k_hTs[cur["i"]]
                kv_block(rope_own, t0, t0)
                if blk < 3:
                    make_hT(x_own, t0 + 512, 1 - cur["i"], store_hT=t0 + 512)
                for cg in range(4):
                    def qcons(ti, ps, kp, cg=cg):
                        headnorm_rope(ps, kp, 4, gq_b, ropeb[:, ti, :], k_rope, qk_bf[:, 0:512], k_qkbf, W)
                        pt, pk = next_pt()
                        for hh in range(4):
                            trn(pt[:, hh, :], qk_bf[:, hh * 128:(hh + 1) * 128], [k_qkbf], [pk], hh == 3)
                        evac(QTst[:, cg * 4:(cg + 1) * 4, ti * 128:(ti + 1) * 128], pt[:, 0:4, :], [pk], [k_QTst])
                    proj_tm("w_in", 0, C_Q + cg * 512, qcons)
                dma(GPD, QT_d[:, :, t0:t0 + 512].rearrange("h p t -> p h t"), QTst[:], [k_QTst], [k_QTd])
                i_block(store_t0=t0)
                for cg in range(4):
                    def gcons(ti, ps, kp, cg=cg, t0=t0):
                        gev, k_gev = next_gev()
                        act(gev[:], ps, AF.Silu, [kp], [k_gev])
                        dma(GPD, g_d[t0 + ti * 128:t0 + (ti + 1) * 128, cg * 512:(cg + 1) * 512], gev[:], [k_gev], [k_gd])
                    proj_tm("w_in", 0, C_G + cg * 512, gcons)
                for c0, dst, kd in ((C_GA, gA_d, k_gAd), (C_GR, gR_d, k_gRd)):
                    for cg in range(4):
                        def gacons(j, ps, kp, cg=cg, dst=dst, kd=kd, t0=t0):
                            gev, k_gev = next_gev()
                            act(gev[:], ps, AF.Sigmoid, [kp], [k_gev])
                            r0 = cg * 512 + j * 128
                            dma(GPD, dst[r0:r0 + 128, t0:t0 + 512], gev[:], [k_gev], [kd])
                        proj_fm("w_in", 0, c0 + cg * 512, gacons)
                for cg in range(4):
                    wq, wqk = load_w(wbufs, wks, "w_in", 0, 16, C_QR + cg * 512)
                    wf, wfk = load_w(wbufs, wks, "w_in", 0, 16, C_FF + cg * 512)
                    for j in range(4):
                        h = cg * 4 + j
                        ps, kp = next_pb()
                        mm_group(ps[:], kp, [(wq[:, kc, j * 128:(j + 1) * 128], hT[:, kc, :]) for kc in range(16)], [k_hT, wqk])
                        act(sqT[:], ps[:], AF.Silu, [kp], [k_sqT])
                        dma(GPD, sq_d[h, :, t0:t0 + 512], sqT[:], [k_sqT], [k_sqd])
                        ps2, kp2 = next_pb()
                        mm_group(ps2[:], kp2, [(wf[:, kc, j * 128:(j + 1) * 128], hT[:, kc, :]) for kc in range(16)], [k_hT, wfk])
                        own_head(ps2[:], kp2, 0, h, t0, False)
                cur["i"] = 1 - cur["i"]

        if stop_after >= 3:
            for h in range(16):
                evac(Sbf[:, h, :], Sst[:, h, :], [k_S[h]], [k_Sbf[h]])
            for blk in range(3, -1, -1):
                t0 = blk * 512
                cur["i"] = blk % 2
                hT, k_hT = hTs[cur["i"]], k_hTs[cur["i"]]
                dma(SPD, hT[:], hT_d[:, :, t0:t0 + 512], [k_hTd], [k_hT])
                dma(SPD, vblk[:], v_d[t0:t0 + 512, :].rearrange("(ti p) c -> p ti c", p=128), [k_vd], [k_vblk])
                for cg in range(4):
                    wf, wfk = load_w(wbufs, wks, "w_in", 0, 16, C_FB + cg * 512)
                    for j in range(4):
                        h = cg * 4 + j
                        dma(SPD, sqT[:], sq_d[h, :, t0:t0 + 512], [k_sqd], [k_sqT])
                        ps2, kp2 = next_pb()
                        mm_group(ps2[:], kp2, [(wf[:, kc, j * 128:(j + 1) * 128], hT[:, kc, :]) for kc in range(16)], [k_hT, wfk])
                        own_head(ps2[:], kp2, 1, h, t0, True)
        cx.barrier()
        run_block()
    if stop_after < 4:
        return nc, es
    es2 = contextlib.ExitStack()
    with es2:
        KTs = sb("KTs", [128, S], BF16); k_KTs = Tk()
        Vs = sb("Vs", [128, 64, 128], BF16); k_Vs = Tk()
        QTb = [sb(f"QTb{i}", [128, 512], BF16) for i in range(2)]; k_QTb = [Tk(), Tk()]
        Pb = [sb(f"Pb{i}", [128, 512], BF16) for i in range(3)]; k_Pb = [Tk(), Tk(), Tk()]
        ones_bf = sb("ones_bf", [128, 128], BF16)
        rs = sb("rs", [128, 512], F32); k_rs = Tk()
        oab = [sb(f"oab{i}", [128, 512], BF16) for i in range(2)]; k_oab = [Tk(), Tk()]
        cx.op(DVE, lambda e: e.memset(ones_bf[:], 1.0), [], [k_const])
        scale = float(HD) ** -0.5
        it = 0
        for kh in range(NKVH):
            dma(SPD, KTs[:], KT_d[kh], [k_KTd], [k_KTs])
            dma(SPD, Vs[:], V_d[:, kh * 128:(kh + 1) * 128].rearrange("(kt p) c -> p kt c", p=128), [k_Vd], [k_Vs])
            for qh in range(4):
                head = kh * 4 + qh
                for qb in range(4):
                    qt, kq = QTb[it % 2], k_QTb[it % 2]
                    dma(SPD, qt[:], QT_d[head, :, qb * 512:(qb + 1) * 512], [k_QTd], [kq])
                    OT, kOT = PB[2], PBk[2]
                    SM, kSM = PB[3], PBk[3]

                    def smm(kt):
                        mm(PB[kt % 2][:], KTs[:, kt * 128:(kt + 1) * 128], qt[:], True, True, [k_KTs, kq], [PBk[kt % 2]], True)
                    smm(0)
                    smm(1)
                    for kt in range(64):
                        pb_, kpb = Pb[kt % 3], k_Pb[kt % 3]
                        act(pb_[:], PB[kt % 2][:], AF.Exp, [PBk[kt % 2]], [kpb], scale=scale)
                        mm(OT[:], Vs[:, kt, :], pb_[:], kt == 0, kt == 63, [k_Vs, kpb], [kOT], kt == 63)
                        mm(SM[:], ones_bf[:], pb_[:], kt == 0, kt == 63, [k_const, kpb], [kSM], kt == 63)
                        if kt + 2 < 64:
                            smm(kt + 2)
                    recip(rs[:], SM[:], [kSM], [k_rs])
                    ob, kob = oab[it % 2], k_oab[it % 2]
                    tt(DVE, ob[:], OT[:], rs[:], ALU.mult, [kOT, k_rs], [kob])
                    dma(GPD, oaT_d[head * 128:(head + 1) * 128, qb * 512:(qb + 1) * 512], ob[:], [kob], [k_oaTd])
                    it += 1
        cx.barrier()
        run_block()

    if stop_after < 5:
        return nc, es
    es2 = contextlib.ExitStack()
    with es2:
        W = NS()
        W.hb = sb("hb5", [128, D], BF16); W.k_hb = Tk()
        W.stat = sb("stat5", [128, 4], F32); W.k_stat = Tk()
        epst = sb("epst5", [128, 1], F32)
        gmlp_b = sb("gmlp_b", [128, D], F32)
        gple_b = sb("gple_b", [128, D], F32)
        gfin_b = sb("gfin_b", [128, D], F32)
        xres = sb("xres", [128, 4, D], F32); k_x = [Tk() for _ in range(4)]
        actA = sb("actA", [128, 16, 512], BF16); k_actA = Tk()
        actB = sb("actB", [128, 16, 512], BF16); k_actB = Tk()
        gAb = [sb(f"gAb{i}", [128, 512], BF16) for i in range(2)]; k_gAb = [Tk(), Tk()]
        gRb = [sb(f"gRb{i}", [128, 512], BF16) for i in range(2)]; k_gRb = [Tk(), Tk()]
        mixT = sb("mixT", [128, 16, 512], BF16); k_mixT = Tk()
        uT, k_uT = actB, k_actB
        wbufs = [sb(f"wc{i}", [128, 16, 512], BF16) for i in range(2)]; wks = [Tk(), Tk()]
        t1 = sb("t1_5", [128, 512], F32); k_t1 = Tk()
        t2 = sb("t2_5", [128, 512], F32); k_t2 = Tk()
        pin = sb("pin", [128, PLE], F32); k_pin = Tk()
        pbf = sb("pbf", [128, PLE], BF16); k_pbf = Tk()
        pT = sb("pT", [128, 2, 512], BF16); k_pT = Tk()
        outb = sb("outb", [128, D], F32); k_outb = Tk()
        cx.op(DVE, lambda e: e.memset(epst[:], EPS), [], [k_const])
        dma(SPD, gmlp_b[:], g_mlp.partition_broadcast(128), [], [k_const])
        dma(SPD, gple_b[:], g_ple.partition_broadcast(128), [], [k_const])
        dma(SPD, gfin_b[:], g_final.partition_broadcast(128), [], [k_const])

        def load_act(dst, kdst, src, ksrc, t0):
            dma(GPD, dst[:], src[:, t0:t0 + 512].rearrange("(kc p) t -> p kc t", p=128), [ksrc], [kdst])

        for blk in range(4):
            t0 = blk * 512
            for ti in range(4):
                dma(SPD, xres[:, ti, :], x_own[t0 + ti * 128:t0 + (ti + 1) * 128, :], [], [k_x[ti]])
            load_act(actA, k_actA, oaT_d, k_oaTd, t0)
            load_act(actB, k_actB, orT_d, k_orTd, t0)
            for cg in range(4):
                wa, wak = load_w(wbufs, wks, "w_o_attn", 0, 16, cg * 512)
                wr, wrk = load_w(wbufs, wks, "w_o_hgrn", 0, 16, cg * 512)
                for j in range(4):
                    dc = cg * 4 + j
                    pa, kpa = next_pb()
                    mm_group(pa[:], kpa, [(wa[:, kc, j * 128:(j + 1) * 128], actA[:, kc, :]) for kc in range(16)], [k_actA, wak])
                    pr, kpr = next_pb()
                    mm_group(pr[:], kpr, [(wr[:, kc, j * 128:(j + 1) * 128], actB[:, kc, :]) for kc in range(16)], [k_actB, wrk])
                    ga, kga, gr, kgr = gAb[dc % 2], k_gAb[dc % 2], gRb[dc % 2], k_gRb[dc % 2]
                    dma(SPD, ga[:], gA_d[dc * 128:(dc + 1) * 128, t0:t0 + 512], [k_gAd], [kga])
                    dma(SPD, gr[:], gR_d[dc * 128:(dc + 1) * 128, t0:t0 + 512], [k_gRd], [kgr])
                    tt(DVE, t1[:], pa[:], ga[:], ALU.mult, [kpa, kga], [k_t1])
                    tt(DVE, t2[:], pr[:], gr[:], ALU.mult, [kpr, kgr], [k_t2])
                    tt(POOL, mixT[:, dc, :], t1[:], t2[:], ALU.add, [k_t1, k_t2], [k_mixT])
            for cg in range(4):
                wo, wok = load_w(wbufs, wks, "w_out", 0, 16, cg * 512)
                for ti in range(4):
                    ps, kp = next_pb()
                    mm_group(ps[:], kp, [(mixT[:, kc, ti * 128:(ti + 1) * 128], wo[:, kc, :]) for kc in range(16)], [k_mixT, wok])
                    tt(DVE, xres[:, ti, cg * 512:(cg + 1) * 512], xres[:, ti, cg * 512:(cg + 1) * 512], ps[:], ALU.add,
                       [k_x[ti], kp], [k_x[ti]])
            for ti in range(4):
                norm_tile(xres[:, ti, :], k_x[ti], gmlp_b, actA, k_actA, ti, W)
            for fq in range(4):
                for fg in range(4):
                    wu, wuk = load_w(wbufs, wks, "w_up", 0, 16, fq * 2048 + fg * 512)
                    for j in range(4):
                        ps, kp = next_pb()
                        mm_group(ps[:], kp, [(wu[:, kc, j * 128:(j + 1) * 128], actA[:, kc, :]) for kc in range(16)], [k_actA, wuk])
                        act(t1[:], ps[:], AF.Square, [kp], [k_t1])
                        stt(uT[:, fg * 4 + j, :], ps[:], 0.0, t1[:], ALU.is_gt, ALU.mult, [kp, k_t1], [k_uT])
                for cg in range(4):
                    wd, wdk = load_w(wbufs, wks, "w_down", fq * 2048, 16, cg * 512)
                    for ti in range(4):
                        ps, kp = next_pb()
                        mm_group(ps[:], kp, [(uT[:, kc, ti * 128:(ti + 1) * 128], wd[:, kc, :]) for kc in range(16)], [k_uT, wdk])
                        tt(DVE, xres[:, ti, cg * 512:(cg + 1) * 512], xres[:, ti, cg * 512:(cg + 1) * 512], ps[:], ALU.add,
                           [k_x[ti], kp], [k_x[ti]])
            for ti in range(4):
                norm_tile(xres[:, ti, :], k_x[ti], gple_b, actA, k_actA, ti, W)
                dma(SPD, pin[:], p_own[t0 + ti * 128:t0 + (ti + 1) * 128, :], [], [k_pin])
                evac(pbf[:], pin[:], [k_pin], [k_pbf], eng=DVE)
                pt, pk = next_pt()
                trn(pt[:, 0, :], pbf[:, 0:128], [k_pbf], [pk], False)
                trn(pt[:, 1, :], pbf[:, 128:256], [k_pbf], [pk], True)
                evac(pT[:, :, ti * 128:(ti + 1) * 128], pt[:, 0:2, :], [pk], [k_pT])
            for cg in range(4):
                wg, wgk = load_w(wbufs, wks, "w_ple_gate", 0, 16, cg * 512)
                wp, wpk = load_w(wbufs, wks, "w_ple", 0, 2, cg * 512)
                for ti in range(4):
                    ps, kp = next_pb()
                    mm_group(ps[:], kp, [(actA[:, kc, ti * 128:(ti + 1) * 128], wg[:, kc, :]) for kc in range(16)], [k_actA, wgk])
                    act(t1[:], ps[:], AF.Sigmoid, [kp], [k_t1])
                    ps2, kp2 = next_pb()
                    mm_group(ps2[:], kp2, [(pT[:, kc, ti * 128:(ti + 1) * 128], wp[:, kc, :]) for kc in range(2)], [k_pT, wpk])
                    tt(DVE, t2[:], ps2[:], t1[:], ALU.mult, [kp2, k_t1], [k_t2])
                    tt(POOL, xres[:, ti, cg * 512:(cg + 1) * 512], xres[:, ti, cg * 512:(cg + 1) * 512], t2[:], ALU.add,
                       [k_x[ti], k_t2], [k_x[ti]])
            for ti in range(4):
                act(W.hb[:], xres[:, ti, :], AF.Square, [k_x[ti]], [W.k_hb, W.k_stat], accum_out=W.stat[:, 0:1])
                act(W.stat[:, 1:2], W.stat[:, 0:1], AF.Ln, [W.k_stat, k_const], [W.k_stat], scale=1.0 / D, bias=epst[:, 0:1])
                act(W.stat[:, 2:3], W.stat[:, 1:2], AF.Exp, [W.k_stat], [W.k_stat], scale=-0.5)
                stt(outb[:], xres[:, ti, :], W.stat[:, 2:3], gfin_b[:], ALU.mult, ALU.mult, [k_x[ti], W.k_stat, k_const], [k_outb])
                dma(GPD, out[t0 + ti * 128:t0 + (ti + 1) * 128, :], outb[:], [k_outb], [k_out])
        cx.barrier()
        run_block()
    return nc, es


def _rope_table(pos):
    inv = (10000.0 ** (-np.arange(0, 64, 2, dtype=np.float32) / np.float32(64))).astype(np.float32)
    row = (pos // 64).astype(np.float32)[:, None] * inv[None, :]
    col = (pos % 64).astype(np.float32)[:, None] * inv[None, :]
    return np.concatenate([np.cos(row), np.cos(col), np.sin(row), np.sin(col)], axis=1).astype(np.float32)


def _consts():
    s = np.arange(128)[:, None]
    t = np.arange(128)[None, :]
    same = (s // 64) == (t // 64)
    tt_ = np.arange(512)
    rf = np.ones((128, 512), np.float32); rf[:, tt_ % 64 == 0] = 0.0
    rb = np.ones((128, 512), np.float32); rb[:, tt_ % 64 == 63] = 0.0
    return {
        "c_ident": np.eye(128, dtype=np.float32).astype(ml_dtypes.bfloat16),
        "c_maskF": (same & (s <= t)).astype(np.float32),
        "c_maskB": (same & (s >= t)).astype(np.float32),
        "c_resetF": rf, "c_resetB": rb,
    }


def make_in_maps(x, p, g_mix, w_in, g_q, g_k, w_o_attn, hgrn_lb, g_hgrn, w_o_hgrn, w_out,
                 g_mlp, w_up, w_down, g_ple, w_ple_gate, w_ple, g_final):
    f = lambda a: np.ascontiguousarray(np.asarray(a, dtype=np.float32))
    x = f(x); p = f(p); w_in0 = f(w_in[0]); hl = f(hgrn_lb)
    shared = dict(
        w_in=w_in0, g_mix=f(g_mix[0]), g_q=f(g_q[0]), g_k=f(g_k[0]), g_hgrn=f(g_hgrn[0]), g_mlp=f(g_mlp[0]),
        g_ple=f(g_ple[0]), g_final=f(g_final), w_o_attn=f(w_o_attn[0]), w_o_hgrn=f(w_o_hgrn[0]), w_out=f(w_out[0]),
        w_up=f(w_up[0]), w_down=f(w_down[0]), w_ple_gate=f(w_ple_gate[0]), w_ple=f(w_ple[0]),
        lbown=np.ascontiguousarray(hl.transpose(1, 0, 2).reshape(2, 2, 16, 128).transpose(3, 0, 1, 2)),
    )
    shared.update(_consts())
    wf = [np.ascontiguousarray(w_in0[:, C_FF:C_FF + D]), np.ascontiguousarray(w_in0[:, C_FB:C_FB + D])]
    maps = []
    for c in range(8):
        b, qi = c // 4, c % 4
        pos_own = np.arange(qi * NOWN, (qi + 1) * NOWN)
        quarters, dirs, poss = [], [], []
        for g in range(3):
            if g < qi:
                qd, d_ = g, 0
                pos = np.arange(qd * NOWN, (qd + 1) * NOWN)
            else:
                qd, d_ = 3 - (g - qi), 1
                pos = np.arange(qd * NOWN, (qd + 1) * NOWN)[::-1]
            dirs.append(d_); poss.append(pos)
        pos_oth = np.concatenate(poss)
        fl = np.zeros((128, 8), np.float32)
        for g in range(3):
            fl[:, g] = 0.0 if g == qi else 1.0
            fl[:, 3 + g] = 1.0 if g == qi - 1 else 0.0
        fl[:, 6] = 1.0 if qi < 3 else 0.0
        m = dict(shared)
        m.update(
            x_own=np.ascontiguousarray(x[b, pos_own]), x_oth=np.ascontiguousarray(x[b, pos_oth]),
            rope_own=_rope_table(pos_own), rope_oth=_rope_table(pos_oth),
            p_own=np.ascontiguousarray(p[0, b, pos_own]),
            w_fsel=np.concatenate([wf[d_] for d_ in dirs], axis=0),
            lbsel=np.ascontiguousarray(np.stack([hl[:, d_, :] for d_ in dirs], axis=0).reshape(3, 2, 16, 128).transpose(3, 0, 1, 2)),
            flags=fl,
        )
        maps.append(m)
    return maps


_NC_CACHE = {}


def kernel(**inputs):
    maps = make_in_maps(**inputs)
    if "nc" not in _NC_CACHE:
        _NC_CACHE["nc"] = build_nc()
    nc, _es = _NC_CACHE["nc"]
    res = run_bass_kernel_spmd(nc, maps, core_ids=list(range(8)))
    outp = np.empty((2, S, D), np.float32)
    for c in range(8):
        b, qi = c // 4, c % 4
        outp[b, qi * NOWN:(qi + 1) * NOWN] = res.results[c]["out"]
    return outp
```

```python
import numpy as np
import ml_dtypes
import concourse.bass as bass
import concourse.mybir as mybir
from concourse.bass_utils import run_bass_kernel_spmd

F32 = mybir.dt.float32
BF16 = mybir.dt.bfloat16
AF = mybir.ActivationFunctionType
ALU = mybir.AluOpType
AX = mybir.AxisListType

D = 2048
S = 8192
NOWN = 2048
HD = 128
NQH = 16
NKVH = 4
DFF = 8192
PLE = 256
EPS = 1e-6
DIN = 17408
C_Q, C_K, C_V, C_QR, C_FF, C_FB, C_I, C_G, C_GA, C_GR = 0, 2048, 2560, 3072, 5120, 7168, 9216, 11264, 13312, 15360


class Tk:
    __slots__ = ("w", "r")

    def __init__(self):
        self.w = None
        self.r = {}


class Stream:
    def __init__(self, name):
        self.name = name
        self.prog = []
        self.seen = {}


class Qu:
    def __init__(self, name, sems, stream, dma=False, pe=False):
        self.name = name
        self.sems = sems
        self.stream = stream
        self.dma = dma
        self.pe = pe
        self.n = 0


class Ctx:
    def __init__(self):
        self.queues = []
        self.streams = []

    def _wait(self, q, tok, waits):
        oq, si, val = tok
        if oq is q and q.pe:
            return
        key = (oq.name, si)
        seen = q.stream.seen
        if seen.get(key, 0) >= val:
            return
        seen[key] = val
        waits.append((oq.sems[si], val))

    def op(self, q, fn, reads=(), writes=(), inc=True):
        waits = []
        for t in reads:
            if t.w is not None:
                self._wait(q, t.w, waits)
        for t in writes:
            if t.w is not None:
                self._wait(q, t.w, waits)
            for tok in t.r.values():
                self._wait(q, tok, waits)
        if q.dma:
            K = len(q.sems)
            i = q.n
            si = i % K
            val = 16 * (i // K + 1)
            if i >= K:
                self._wait(q, (q, si, val - 16), waits)
            tok = (q, si, val)
            q.n += 1
            q.stream.prog.append((waits, fn, (q.sems[si], 16)))
        else:
            tok = (q, 0, q.n + 1)
            if inc:
                q.n += 1
                q.stream.prog.append((waits, fn, (q.sems[0], 1)))
            else:
                q.stream.prog.append((waits, fn, None))
        for t in reads:
            t.r[(q.name, tok[1])] = tok
        for t in writes:
            t.w = tok
            t.r = {}
        return tok

    def last_tokens(self, q):
        if q.dma:
            K = len(q.sems)
            out = []
            for si in range(K):
                cnt = (q.n - si + K - 1) // K if q.n > si else 0
                if cnt > 0:
                    out.append((q, si, 16 * cnt))
            return out
        return [(q, 0, q.n)] if q.n > 0 else []

    def barrier(self):
        for st in self.streams:
            waits = []
            for oq in self.queues:
                for tok in self.last_tokens(oq):
                    key = (oq.name, tok[1])
                    if st.seen.get(key, 0) >= tok[2]:
                        continue
                    st.seen[key] = tok[2]
                    waits.append((oq.sems[tok[1]], tok[2]))
            if waits:
                st.prog.append((waits, None, None))


def replay(stream, eng):
    for waits, fn, inc in stream.prog:
        for sem, val in waits:
            eng.wait_ge(sem, val)
        if fn is None:
            continue
        ins = fn(eng)
        if inc is not None:
            ins.then_inc(inc[0], inc[1])
    stream.prog = []


def build_nc(stop_after=99, debug=False):
    nc = bass.Bass("TRN2", target_bir_lowering=False)
    import contextlib
    es = contextlib.ExitStack()

    def din(name, shape, dt=F32):
        return nc.dram_tensor(name, list(shape), dt, kind="ExternalInput").ap()

    def dscr(name, shape, dt):
        return nc.dram_tensor(name, list(shape), dt, kind=("ExternalOutput" if debug else "Internal")).ap()

    x_own = din("x_own", [NOWN, D])
    x_oth = din("x_oth", [3 * NOWN, D])
    rope_own = din("rope_own", [NOWN, 128])
    rope_oth = din("rope_oth", [3 * NOWN, 128])
    p_own = din("p_own", [NOWN, PLE])
    w_in = din("w_in", [D, DIN])
    w_fsel = din("w_fsel", [3 * D, D])
    lbsel = din("lbsel", [128, 3, 2, 16])
    lbown = din("lbown", [128, 2, 2, 16])
    flags = din("flags", [128, 8])
    g_mix = din("g_mix", [D])
    g_q = din("g_q", [HD])
    g_k = din("g_k", [HD])
    g_hgrn = din("g_hgrn", [HD])
    g_mlp = din("g_mlp", [D])
    g_ple = din("g_ple", [D])
    g_final = din("g_final", [D])
    w_o_attn = din("w_o_attn", [D, D])
    w_o_hgrn = din("w_o_hgrn", [D, D])
    w_out = din("w_out", [D, D])
    w_up = din("w_up", [D, DFF])
    w_down = din("w_down", [DFF, D])
    w_ple_gate = din("w_ple_gate", [D, D])
    w_ple = din("w_ple", [PLE, D])
    c_ident = din("c_ident", [128, 128], BF16)
    c_maskF = din("c_maskF", [128, 128])
    c_maskB = din("c_maskB", [128, 128])
    c_resetF = din("c_resetF", [128, 512])
    c_resetB = din("c_resetB", [128, 512])
    out = nc.dram_tensor("out", [NOWN, D], F32, kind="ExternalOutput").ap()

    KT_d = dscr("KT_d", [NKVH, 128, S], BF16)
    V_d = dscr("V_d", [S, NKVH * HD], BF16)
    QT_d = dscr("QT_d", [NQH, 128, NOWN], BF16)
    hT_d = dscr("hT_d", [128, 16, NOWN], BF16)
    sq_d = dscr("sq_d", [NQH, 128, NOWN], BF16)
    v_d = dscr("v_d", [NOWN, D], BF16)
    g_d = dscr("g_d", [NOWN, D], BF16)
    ofw_d = dscr("ofw_d", [NQH, NOWN, 128], F32)
    gA_d = dscr("gA_d", [D, NOWN], BF16)
    gR_d = dscr("gR_d", [D, NOWN], BF16)
    orT_d = dscr("orT_d", [D, NOWN], BF16)
    oaT_d = dscr("oaT_d", [D, NOWN], BF16)
    Sdbg_d = dscr("Sdbg_d", [2, 128, 16 * 128], F32)
    WSRC = {"w_in": (w_in, [D, DIN]), "w_fsel": (w_fsel, [3 * D, D]), "w_o_attn": (w_o_attn, [D, D]),
            "w_o_hgrn": (w_o_hgrn, [D, D]), "w_out": (w_out, [D, D]), "w_up": (w_up, [D, DFF]),
            "w_down": (w_down, [DFF, D]), "w_ple_gate": (w_ple_gate, [D, D]), "w_ple": (w_ple, [PLE, D])}
    WBF = {n: nc.dram_tensor("bf_" + n, shp, BF16, kind="Internal").ap() for n, (_, shp) in WSRC.items()}
    WTK = {}

    cx = Ctx()

    def mkq(name, nsem, stream, dma=False, pe=False):
        sems = [es.enter_context(nc.semaphore(f"{name}_s{i}")) for i in range(nsem)]
        q = Qu(name, sems, stream, dma=dma, pe=pe)
        cx.queues.append(q)
        return q

    st_pe, st_act, st_dve, st_pool, st_sp = (Stream(n) for n in ("pe", "act", "dve", "pool", "sp"))
    cx.streams = [st_pe, st_act, st_dve, st_pool, st_sp]
    PE = mkq("pe", 1, st_pe, pe=True)
    ACT = mkq("act", 1, st_act)
    DVE = mkq("dve", 1, st_dve)
    POOL = mkq("pool", 1, st_pool)
    SPD = mkq("spd", 8, st_sp, dma=True)
    GPD = mkq("gpd", 8, st_pool, dma=True)

    def run_block():
        with nc.Block() as block:
            @block.tensor
            def _(e):
                replay(st_pe, e)

            @block.scalar
            def _(e):
                replay(st_act, e)

            @block.vector
            def _(e):
                replay(st_dve, e)

            @block.gpsimd
            def _(e):
                replay(st_pool, e)

            @block.sync
            def _(e):
                replay(st_sp, e)

    def sb(name, shape, dt):
        return es2.enter_context(nc.sbuf_tensor(name, list(shape), dt))

    PB = [es.enter_context(nc.psum_tensor(f"pb{i}", [128, 512], F32)) for i in range(6)]
    PBT = [es.enter_context(nc.psum_tensor(f"pbt{i}", [128, 1024], BF16)) for i in range(2)]
    PBk = [Tk() for _ in range(6)]
    PBTk = [Tk() for _ in range(2)]

    def psb(name, shape, dt):
        return es.enter_context(nc.sbuf_tensor(name, list(shape), dt))

    ident = psb("ident", [128, 128], BF16)
    maskF = psb("maskF", [128, 128], F32)
    maskB = psb("maskB", [128, 128], F32)
    resetF = psb("resetF", [128, 512], F32)
    resetB = psb("resetB", [128, 512], F32)
    flg = psb("flg", [128, 8], F32)
    gq_b = psb("gq_b", [128, HD], F32)
    gk_b = psb("gk_b", [128, HD], F32)
    gh_b = psb("gh_b", [128, HD], F32)
    Sst = psb("Sst", [128, 16, 128], F32)
    Sfw = psb("Sfw", [128, 16, 128], F32)
    Sbf = psb("Sbf", [128, 16, 128], BF16)
    lb_own = psb("lb_own", [128, 2, 16], F32)
    oml_own = psb("oml_own", [128, 2, 16], F32)
    nml_own = psb("nml_own", [128, 2, 16], F32)
    lb_sel = psb("lb_sel", [128, 3, 16], F32)
    oml_sel = psb("oml_sel", [128, 3, 16], F32)
    nml_sel = psb("nml_sel", [128, 3, 16], F32)
    k_const = Tk()
    k_S = [Tk() for _ in range(16)]
    k_Sfw = Tk()
    k_out = Tk()
    k_Sbf = [Tk() for _ in range(16)]

    es2 = es
    if True:
        lbr = psb("lbr", [128, 2, 2, 16], F32)
        lbs = psb("lbs", [128, 3, 2, 16], F32)
        tmpc = psb("tmpc", [128, 5, 16], F32)
        k_tmp = Tk()
        for dst, src in ((ident, c_ident), (maskF, c_maskF), (maskB, c_maskB), (resetF, c_resetF),
                         (resetB, c_resetB), (flg, flags)):
            cx.op(SPD, lambda e, d=dst, s=src: e.dma_start(out=d[:], in_=s), writes=[k_const])
        for dst, src in ((gq_b, g_q), (gk_b, g_k), (gh_b, g_hgrn)):
            cx.op(SPD, lambda e, d=dst, s=src: e.dma_start(out=d[:], in_=s.partition_broadcast(128)),
                  writes=[k_const])
        cx.op(SPD, lambda e: e.dma_start(out=lbr[:], in_=lbown), writes=[k_tmp])
        cx.op(SPD, lambda e: e.dma_start(out=lbs[:], in_=lbsel), writes=[k_tmp])

        def lbcalc(raw, n, lb_t, oml_t, nml_t):
            d_ = tmpc[:, 0:n, :]
            cx.op(DVE, lambda e: e.tensor_tensor(out=d_, in0=raw[:, :, 0, :], in1=raw[:, :, 1, :], op=ALU.subtract),
                  reads=[k_tmp], writes=[k_tmp])
            cx.op(ACT, lambda e: e.activation(out=lb_t[:], in_=d_, func=AF.Sigmoid), reads=[k_tmp], writes=[k_const])
            cx.op(DVE, lambda e: e.tensor_scalar(out=oml_t[:], in0=lb_t[:], scalar1=-1.0, scalar2=1.0,
                                                 op0=ALU.mult, op1=ALU.add), reads=[k_const], writes=[k_const])
            cx.op(DVE, lambda e: e.tensor_scalar(out=nml_t[:], in0=lb_t[:], scalar1=1.0, scalar2=-1.0,
                                                 op0=ALU.mult, op1=ALU.add), reads=[k_const], writes=[k_const])

        lbcalc(lbr, 2, lb_own, oml_own, nml_own)
        lbcalc(lbs, 3, lb_sel, oml_sel, nml_sel)
        cx.op(DVE, lambda e: e.memset(Sst[:], 0.0), writes=k_S)
        cx.op(DVE, lambda e: e.memset(Sfw[:], 0.0), writes=[k_Sfw])
        cx.op(POOL, lambda e: e.memset(Sbf[:], 0.0), writes=k_Sbf)

        def conv(name, r0, nr, c0, ncols):
            src, _ = WSRC[name]
            tk = Tk()
            WTK[(name, r0, c0)] = (tk, nr, ncols)
            for cc in range(c0, c0 + ncols, 2048):
                w_ = min(2048, c0 + ncols - cc)
                cx.op(GPD, lambda e, cc=cc, w_=w_: e.dma_start(out=WBF[name][r0:r0 + nr, cc:cc + w_],
                                                               in_=src[r0:r0 + nr, cc:cc + w_]), [], [tk])
        conv("w_in", 0, D, C_K, 1024)
        conv("w_in", 0, D, C_I, 2048)
        for g in range(3):
            conv("w_fsel", g * D, D, 0, 2048)
        later = [("w_in", 0, D, c0, 2048) for c0 in (C_Q, C_G, C_GA, C_GR, C_QR, C_FF, C_FB)]
        later += [("w_o_attn", 0, D, 0, D), ("w_o_hgrn", 0, D, 0, D), ("w_out", 0, D, 0, D)]
        later += [("w_up", 0, D, q_ * 2048, 2048) for q_ in range(4)]
        later += [("w_down", q_ * 2048, 2048, 0, 2048) for q_ in range(4)]
        later += [("w_ple_gate", 0, D, 0, D), ("w_ple", 0, PLE, 0, D)]

        def issue_convs(n):
            for _ in range(n):
                if later:
                    conv(*later.pop(0))
        run_block()

    state = {"pb": 0, "w": 0, "pt": 0, "ev": 0}

    def act(out, in_, func, r, w, **kw):
        return cx.op(ACT, lambda e: e.activation(out=out, in_=in_, func=func, **kw), r, w)

    def tt(q, out, in0, in1, op, r, w):
        return cx.op(q, lambda e: e.tensor_tensor(out=out, in0=in0, in1=in1, op=op), r, w)

    def ts(q, out, in0, s1, s2, op0, op1, r, w):
        return cx.op(q, lambda e: e.tensor_scalar(out=out, in0=in0, scalar1=s1, scalar2=s2, op0=op0, op1=op1), r, w)

    def stt(out, in0, scalar, in1, op0, op1, r, w):
        return cx.op(DVE, lambda e: e.scalar_tensor_tensor(out=out, in0=in0, scalar=scalar, in1=in1, op0=op0, op1=op1), r, w)

    def recip(out, in_, r, w):
        return cx.op(DVE, lambda e: e.reciprocal(out=out, in_=in_), r, w)

    def dma(q, out, in_, r, w):
        return cx.op(q, lambda e: e.dma_start(out=out, in_=in_), r, w)

    def mm(out, lhsT, rhs, start, stop, r, w, inc):
        return cx.op(PE, lambda e: e.matmul(out, lhsT, rhs, start=start, stop=stop), r, w, inc=inc)

    def trn(out, in_, r, w, inc):
        return cx.op(PE, lambda e: e.transpose(out, in_, ident[:]), list(r) + [k_const], w, inc=inc)

    def evac(dst, src, r, w, eng=None):
        if eng is None:
            eng = ACT if state["ev"] % 2 == 0 else DVE
            state["ev"] += 1
        if eng is ACT:
            return cx.op(ACT, lambda e: e.copy(out=dst, in_=src), r, w)
        return cx.op(eng, lambda e: e.tensor_copy(out=dst, in_=src), r, w)

    def next_pb():
        i = state["pb"] % 4
        state["pb"] += 1
        return PB[i], PBk[i]

    def next_pt():
        i = state["pt"] % 2
        state["pt"] += 1
        return PBT[i][:, :].rearrange("p (a b) -> p a b", a=8), PBTk[i]

    def load_w(wbufs, wks, name, r0, nkc, c0, ncols=512):
        i = state["w"] % len(wbufs)
        state["w"] += 1
        wb, wk = wbufs[i], wks[i]
        tk = None
        for (n_, rr, cc), (t_, nr, ncl) in WTK.items():
            if n_ == name and rr <= r0 and r0 + nkc * 128 <= rr + nr and cc <= c0 and c0 + ncols <= cc + ncl:
                tk = t_
        assert tk is not None, (name, r0, c0)
        src = WBF[name][r0:r0 + nkc * 128, c0:c0 + ncols].rearrange("(kc p) n -> p kc n", p=128)
        dma(SPD, wb[:, 0:nkc, 0:ncols], src, [tk], [wk])
        return wb, wk

    def mm_group(out_ap, out_k, pairs, reads):
        n = len(pairs)
        for i, (l, r_) in enumerate(pairs):
            mm(out_ap, l, r_, i == 0, i == n - 1, reads, [out_k], i == n - 1)

    class NS:
        pass

    def norm_tile(x_ap, k_x, gbt, hT, k_hT, ti, W, scale_d=D):
        act(W.hb[:], x_ap, AF.Square, [k_x], [W.k_hb, W.k_stat], accum_out=W.stat[:, 0:1])
        act(W.stat[:, 1:2], W.stat[:, 0:1], AF.Sqrt, [W.k_stat, k_const], [W.k_stat], scale=1.0 / scale_d, bias=epst[:, 0:1])
        recip(W.stat[:, 2:3], W.stat[:, 1:2], [W.k_stat], [W.k_stat])
        stt(W.hb[:], x_ap, W.stat[:, 2:3], gbt[:], ALU.mult, ALU.mult, [k_x, W.k_stat, k_const], [W.k_hb])
        for half in range(2):
            pt, pk = next_pt()
            for j in range(8):
                kc = half * 8 + j
                trn(pt[:, j, :], W.hb[:, kc * 128:(kc + 1) * 128], [W.k_hb], [pk], j == 7)
            evac(hT[:, half * 8:(half + 1) * 8, ti * 128:(ti + 1) * 128], pt, [pk], [k_hT])

    def headnorm_rope(ps_ap, k_ps, nh, gb, ropet, k_rope, dst, k_dst, W):
        n = nh * 128
        sq = W.sq[:, 0:n]
        act(sq, ps_ap, AF.Square, [k_ps], [W.k_sq])
        cx.op(DVE, lambda e: e.tensor_reduce(out=W.hs[:, 0:nh], in_=sq.rearrange("p (h d) -> p h d", h=nh),
                                             axis=AX.X, op=ALU.add), [W.k_sq], [W.k_hs])
        act(W.hs[:, 16:16 + nh], W.hs[:, 0:nh], AF.Sqrt, [W.k_hs, k_const], [W.k_hs], scale=1.0 / HD, bias=epst[:, 0:1])
        recip(W.hs[:, 32:32 + nh], W.hs[:, 16:16 + nh], [W.k_hs], [W.k_hs])
        kn = W.kn[:, 0:n]
        kn3 = kn.rearrange("p (h d) -> p h d", h=nh)
        tt(DVE, kn3, ps_ap.rearrange("p (h d) -> p h d", h=nh),
           W.hs[:, 32:32 + nh].unsqueeze(2).broadcast_to([128, nh, 128]), ALU.mult, [k_ps, W.k_hs], [W.k_kn])
        tt(POOL, kn3, kn3, gb[:].unsqueeze(1).broadcast_to([128, nh, 128]), ALU.mult, [W.k_kn, k_const], [W.k_kn])
        x5 = kn.rearrange("p (h a b c) -> p h a b c", h=nh, a=2, b=2)
        d5 = dst.rearrange("p (h a b c) -> p h a b c", h=nh, a=2, b=2)
        x1, x2 = x5[:, :, :, 0, :], x5[:, :, :, 1, :]
        Cc = ropet[:, 0:64].rearrange("p (a c) -> p a c", a=2).unsqueeze(1).broadcast_to([128, nh, 2, 32])
        Sn = ropet[:, 64:128].rearrange("p (a c) -> p a c", a=2).unsqueeze(1).broadcast_to([128, nh, 2, 32])
        m = nh * 64
        t1 = W.t1[:, 0:m].rearrange("p (h a c) -> p h a c", h=nh, a=2)
        t2 = W.t2[:, 0:m].rearrange("p (h a c) -> p h a c", h=nh, a=2)
        t3 = W.t3[:, 0:m].rearrange("p (h a c) -> p h a c", h=nh, a=2)
        t4 = W.t4[:, 0:m].rearrange("p (h a c) -> p h a c", h=nh, a=2)
        tt(POOL, t1, x1, Cc, ALU.mult, [W.k_kn, k_rope], [W.k_t])
        tt(POOL, t2, x2, Sn, ALU.mult, [W.k_kn, k_rope], [W.k_t])
        tt(POOL, t3, x1, Sn, ALU.mult, [W.k_kn, k_rope], [W.k_t])
        tt(POOL, t4, x2, Cc, ALU.mult, [W.k_kn, k_rope], [W.k_t])
        tt(POOL, d5[:, :, :, 0, :], t1, t2, ALU.subtract, [W.k_t], [k_dst])
        tt(POOL, d5[:, :, :, 1, :], t3, t4, ALU.add, [W.k_t], [k_dst])

    es2 = contextlib.ExitStack()
    with es2:
        W = NS()
        W.hb = sb("hb", [128, D], BF16); W.k_hb = Tk()
        W.stat = sb("stat", [128, 4], F32); W.k_stat = Tk()
        W.sq = sb("wsq", [128, 512], F32); W.k_sq = Tk()
        W.hs = sb("whs", [128, 48], F32); W.k_hs = Tk()
        W.kn = sb("wkn", [128, 512], F32); W.k_kn = Tk()
        W.t1 = sb("wt1", [128, 256], F32); W.t2 = sb("wt2", [128, 256], F32)
        W.t3 = sb("wt3", [128, 256], F32); W.t4 = sb("wt4", [128, 256], F32); W.k_t = Tk()
        epst = sb("epst", [128, 1], F32)
        gmix_b = sb("gmix_b", [128, D], F32)
        xt = [sb(f"xt{i}", [128, D], F32) for i in range(2)]; k_xt = [Tk(), Tk()]
        hTs = [sb(f"hT{i}", [128, 16, 512], BF16) for i in range(2)]; k_hTs = [Tk(), Tk()]
        cur = {"i": 0}
        wbufs = [sb(f"wb{i}", [128, 16, 512], BF16) for i in range(2)]; wks = [Tk(), Tk()]
        ropeb = sb("ropeb", [128, 4, 128], F32); k_rope = Tk()
        qk_bf = sb("qk_bf", [128, 512], BF16); k_qkbf = Tk()
        KTst = sb("KTst", [128, 4, 512], BF16); k_KTst = Tk()
        Vst = sb("Vst", [128, 4, 512], BF16); k_Vst = Tk()
        QTst = sb("QTst", [128, 16, 512], BF16); k_QTst = Tk()
        vblk = sb("vblk", [128, 4, 2048], BF16); k_vblk = Tk()
        sig = sb("sig", [128, 512], F32); k_sig = Tk()
        logf = sb("logf", [128, 512], F32); k_logf = Tk()
        kTt = sb("kTt", [128, 512], F32); k_kT = Tk()
        Asc = sb("Asc", [128, 512], F32); k_A = Tk()
        eA = sb("eA", [128, 512], F32); k_eA = Tk()
        enA = sb("enA", [128, 512], F32); k_enA = Tk()
        ones512 = sb("ones512", [128, 512], F32)
        wT = sb("wT", [128, 512], BF16); k_wT = Tk()
        wtm = sb("wtm", [128, 4, 128], BF16); k_wtm = Tk()
        sqT = sb("sqT", [128, 512], BF16); k_sqT = Tk()
        qlo = sb("qlo", [128, 4, 128], BF16); qhi = sb("qhi", [128, 4, 128], BF16); k_ql = Tk()
        scm = sb("scm", [128, 128], BF16); k_scm = Tk()
        Ut = sb("Ut", [128, 128], F32); k_U = Tk()
        Sbf2 = sb("Sbf2", [128, 128], BF16); k_Sbf2 = Tk()
        ebt = sb("ebt", [128, 2], F32); k_eb = Tk()
        ost = sb("ost", [128, 4, 128], F32); k_ost = Tk()
        ofwst = sb("ofwst", [128, 4, 128], F32); k_ofwst = Tk()
        gst = sb("gst", [128, 4, 128], BF16); k_gst = Tk()
        otot = sb("otot", [128, 128], F32); k_otot = Tk()
        orb = sb("orb", [128, 128], BF16); k_orb = Tk()
        orTst = sb("orTst", [128, 512], BF16); k_orTst = Tk()
        gevs = [sb(f"gev{i}", [128, 512], BF16) for i in range(2)]; k_gevs = [Tk(), Tk()]
        gevi = {"i": 0}

        def next_gev():
            i = gevi["i"] % 2
            gevi["i"] += 1
            return gevs[i], k_gevs[i]

        cx.op(DVE, lambda e: e.memset(epst[:], EPS), [], [k_const])
        cx.op(DVE, lambda e: e.memset(ones512[:], 1.0), [], [k_const])
        cx.op(DVE, lambda e: e.memset(Asc[:], 0.0), [], [k_A])
        cx.op(POOL, lambda e: e.memset(qlo[:], 0.0), [], [k_ql])
        cx.op(POOL, lambda e: e.memset(qhi[:], 0.0), [], [k_ql])
        dma(SPD, gmix_b[:], g_mix.partition_broadcast(128), [], [k_const])

        def make_hT(xsrc, t0, slot, store_hT=None):
            hT, k_hT = hTs[slot], k_hTs[slot]
            for ti in range(4):
                x_t, kx = xt[ti % 2], k_xt[ti % 2]
                dma(SPD, x_t[:], xsrc[t0 + ti * 128:t0 + (ti + 1) * 128, :], [], [kx])
                norm_tile(x_t[:], kx, gmix_b, hT, k_hT, ti, W)
            if store_hT is not None:
                dma(GPD, hT_d[:, :, store_hT:store_hT + 512], hT[:], [k_hT], [k_hTd])

        def proj_tm(src2d, r0, c0, consumer):
            wb, wk = load_w(wbufs, wks, src2d, r0, 16, c0)
            hT, k_hT = hTs[cur["i"]], k_hTs[cur["i"]]
            for ti in range(4):
                ps, kp = next_pb()
                mm_group(ps[:], kp, [(hT[:, kc, ti * 128:(ti + 1) * 128], wb[:, kc, :]) for kc in range(16)], [k_hT, wk])
                consumer(ti, ps[:], kp)

        def proj_fm(src2d, r0, c0, consumer):
            wb, wk = load_w(wbufs, wks, src2d, r0, 16, c0)
            hT, k_hT = hTs[cur["i"]], k_hTs[cur["i"]]
            for j in range(4):
                ps, kp = next_pb()
                mm_group(ps[:], kp, [(wb[:, kc, j * 128:(j + 1) * 128], hT[:, kc, :]) for kc in range(16)], [k_hT, wk])
                consumer(j, ps[:], kp)

        def kv_block(ropesrc, t0, pos):
            dma(SPD, ropeb[:], ropesrc[t0:t0 + 512, :].rearrange("(ti p) c -> p ti c", p=128), [], [k_rope])

            def kcons(ti, ps, kp):
                headnorm_rope(ps, kp, 4, gk_b, ropeb[:, ti, :], k_rope, qk_bf[:, 0:512], k_qkbf, W)
                pt, pk = next_pt()
                for h in range(4):
                    trn(pt[:, h, :], qk_bf[:, h * 128:(h + 1) * 128], [k_qkbf], [pk], h == 3)
                evac(KTst[:, :, ti * 128:(ti + 1) * 128], pt[:, 0:4, :], [pk], [k_KTst])
            proj_tm("w_in", 0, C_K, kcons)
            dma(GPD, KT_d[:, :, pos:pos + 512].rearrange("h p t -> p h t"), KTst[:], [k_KTst], [k_KTd])

            def vcons(ti, ps, kp):
                evac(Vst[:, ti, :], ps, [kp], [k_Vst])
            proj_tm("w_in", 0, C_V, vcons)
            dma(GPD, V_d[pos:pos + 512, :].rearrange("(ti p) c -> p ti c", p=128), Vst[:], [k_Vst], [k_Vd])

        def i_block(store_t0=None):
            for cg in range(4):
                def icons(ti, ps, kp, cg=cg):
                    evac(vblk[:, ti, cg * 512:(cg + 1) * 512], ps, [kp], [k_vblk])
                proj_tm("w_in", 0, C_I + cg * 512, icons)
            if store_t0 is not None:
                dma(GPD, v_d[store_t0:store_t0 + 512, :].rearrange("(ti p) c -> p ti c", p=128), vblk[:], [k_vblk], [k_vd])

        def gates_common(ps, kp, lbt, omlt, nmlt, idx, h):
            act(sig[:], ps, AF.Sigmoid, [kp], [k_sig])
            act(logf[:], sig[:], AF.Ln, [k_sig, k_const], [k_logf], scale=omlt[:, idx, h:h + 1], bias=lbt[:, idx, h:h + 1])
            ts(DVE, kTt[:], sig[:], nmlt[:, idx, h:h + 1], omlt[:, idx, h:h + 1], ALU.mult, ALU.add, [k_sig, k_const], [k_kT])

        wT2 = [wT, sb("wTb", [128, 512], BF16)]; k_wT2 = [k_wT, Tk()]
        ebt2 = [ebt, sb("ebtb", [128, 2], F32)]; k_eb2 = [k_eb, Tk()]

        def so_A(g, cg, j, wb, wk, slot):
            h = cg * 4 + j
            hT, k_hT = hTs[cur["i"]], k_hTs[cur["i"]]
            ps, kp = next_pb()
            mm_group(ps[:], kp, [(wb[:, kc, j * 128:(j + 1) * 128], hT[:, kc, :]) for kc in range(16)], [k_hT, wk])
            gates_common(ps[:], kp, lb_sel, oml_sel, nml_sel, g, h)
            cx.op(DVE, lambda e: e.tensor_tensor_scan(out=Asc[:, 510::-1], data0=ones512[:, 0:511], data1=logf[:, 511:0:-1],
                                                      initial=0.0, op0=ALU.mult, op1=ALU.add), [k_logf, k_const], [k_A])
            act(eA[:], Asc[:], AF.Exp, [k_A], [k_eA])
            tt(POOL, wT2[slot][:], kTt[:], eA[:], ALU.mult, [k_kT, k_eA], [k_wT2[slot]])
            eb = ebt2[slot]
            ts(DVE, eb[:, 0:1], kTt[:, 0:1], -1.0, 1.0, ALU.mult, ALU.add, [k_kT], [k_eb2[slot]])
            tt(DVE, eb[:, 1:2], eb[:, 0:1], eA[:, 0:1], ALU.mult, [k_eb2[slot], k_eA], [k_eb2[slot]])

        def so_B(h, slot):
            pt, pk = next_pt()
            for ti in range(4):
                trn(pt[:, ti, :], wT2[slot][:, ti * 128:(ti + 1) * 128], [k_wT2[slot]], [pk], ti == 3)
            evac(wtm[:], pt[:, 0:4, :], [pk], [k_wtm])
            mm_group(PB[5][:, 0:128], PBk[5], [(wtm[:, ti, :], vblk[:, ti, h * 128:(h + 1) * 128]) for ti in range(4)],
                     [k_wtm, k_vblk])
            stt(Sst[:, h, :], Sst[:, h, :], ebt2[slot][:, 1:2], PB[5][:, 0:128], ALU.mult, ALU.add,
                [k_S[h], k_eb2[slot], PBk[5]], [k_S[h]])

        def state_only_block(g):
            wb = wk = None
            for h in range(17):
                if h < 16:
                    cg, j = divmod(h, 4)
                    if j == 0:
                        wb, wk = load_w(wbufs, wks, "w_fsel", g * D, 16, cg * 512)
                    so_A(g, cg, j, wb, wk, h % 2)
                if h >= 1:
                    so_B(h - 1, (h - 1) % 2)

        def own_head(ps, kp, d, h, t0, final):
            Srun = Sfw if d == 0 else Sst
            kS = k_Sfw if d == 0 else k_S[h]
            gates_common(ps, kp, lb_own, oml_own, nml_own, d, h)
            if d == 0:
                cx.op(DVE, lambda e: e.tensor_tensor_scan(out=Asc[:], data0=resetF[:], data1=logf[:], initial=0.0,
                                                          op0=ALU.mult, op1=ALU.add), [k_logf, k_const], [k_A])
            else:
                cx.op(DVE, lambda e: e.tensor_tensor_scan(out=Asc[:, ::-1], data0=resetB[:, ::-1], data1=logf[:, ::-1],
                                                          initial=0.0, op0=ALU.mult, op1=ALU.add), [k_logf, k_const], [k_A])
            act(eA[:], Asc[:], AF.Exp, [k_A], [k_eA])
            act(enA[:], Asc[:], AF.Exp, [k_A], [k_enA], scale=-1.0)
            sq3 = sqT[:].rearrange("p (t c) -> p t c", t=4)
            eA3 = eA[:].rearrange("p (t c) -> p t c", t=4)
            tt(POOL, qlo[:, :, 0:64], sq3[:, :, 0:64], eA3[:, :, 0:64], ALU.mult, [k_sqT, k_eA], [k_ql])
            tt(POOL, qhi[:, :, 64:128], sq3[:, :, 64:128], eA3[:, :, 64:128], ALU.mult, [k_sqT, k_eA], [k_ql])
            tt(DVE, wT[:], kTt[:], enA[:], ALU.mult, [k_kT, k_enA], [k_wT])
            pt, pk = next_pt()
            for ti in range(4):
                trn(pt[:, ti, :], wT[:, ti * 128:(ti + 1) * 128], [k_wT], [pk], ti == 3)
            evac(wtm[:], pt[:, 0:4, :], [pk], [k_wtm])
            if final:
                dma(SPD, ofwst[:], ofw_d[h, t0:t0 + 512, :].rearrange("(ti p) v -> p ti v", p=128), [k_ofwd], [k_ofwst])
                dma(SPD, gst[:], g_d[t0:t0 + 512, h * 128:(h + 1) * 128].rearrange("(ti p) v -> p ti v", p=128), [k_gd], [k_gst])
            mask = maskF if d == 0 else maskB
            tiles = range(4) if d == 0 else range(3, -1, -1)
            for ti in tiles:
                sc = PB[4][:, 0:128]
                mm(sc, wT[:, ti * 128:(ti + 1) * 128], qlo[:, ti, :], True, False, [k_wT, k_ql], [PBk[4]], False)
                mm(sc, wT[:, ti * 128:(ti + 1) * 128], qhi[:, ti, :], False, True, [k_wT, k_ql], [PBk[4]], True)
                tt(DVE, scm[:], sc, mask[:], ALU.mult, [PBk[4], k_const], [k_scm])
                c1, c2 = (0, 64) if d == 0 else (64, 0)
                e1 = eA[:, ti * 128 + (63 if d == 0 else 64): ti * 128 + (63 if d == 0 else 64) + 1]
                e2 = eA[:, ti * 128 + (127 if d == 0 else 0): ti * 128 + (127 if d == 0 else 0) + 1]
                q1, q2 = (qlo, qhi) if d == 0 else (qhi, qlo)
                dS = PB[5][:, 0:128]
                mm(dS, wtm[c1:c1 + 64, ti, :], vblk[c1:c1 + 64, ti, h * 128:(h + 1) * 128], True, True,
                   [k_wtm, k_vblk], [PBk[5]], True)
                tt(DVE, Ut[:], Srun[:, h, :], dS, ALU.add, [kS, PBk[5]], [k_U])
                ts(POOL, Srun[:, h, :], Ut[:], e1, None, ALU.mult, ALU.bypass, [k_U, k_eA], [kS])
                act(Sbf2[:], Ut[:], AF.Copy, [k_U, k_eA], [k_Sbf2], scale=e1)
                op_ = PB[5][:, 128:256]
                mm(op_, q1[:, ti, :], Sbf[:, h, :], True, False, [k_ql, k_Sbf[h]], [PBk[6]], False)
                mm(op_, q2[:, ti, :], Sbf2[:], False, False, [k_ql, k_Sbf2], [PBk[6]], False)
                mm(op_, scm[:], vblk[:, ti, h * 128:(h + 1) * 128], False, True, [k_scm, k_vblk], [PBk[6]], True)
                dS2 = PB[5][:, 256:384]
                mm(dS2, wtm[c2:c2 + 64, ti, :], vblk[c2:c2 + 64, ti, h * 128:(h + 1) * 128], True, True,
                   [k_wtm, k_vblk], [PBk[7]], True)
                tt(DVE, Ut[:], Srun[:, h, :], dS2, ALU.add, [kS, PBk[7]], [k_U])
                ts(POOL, Srun[:, h, :], Ut[:], e2, None, ALU.mult, ALU.bypass, [k_U, k_eA], [kS])
                act(Sbf[:, h, :], Ut[:], AF.Copy, [k_U, k_eA], [k_Sbf[h]], scale=e2)
                if not final:
                    evac(ost[:, ti, :], op_, [PBk[6]], [k_ost], eng=ACT)
                else:
                    tt(DVE, otot[:], op_, ofwst[:, ti, :], ALU.add, [PBk[6], k_ofwst], [k_otot])
                    act(orb[:], otot[:], AF.Square, [k_otot], [k_orb, W.k_stat], accum_out=W.stat[:, 0:1])
                    act(W.stat[:, 1:2], W.stat[:, 0:1], AF.Sqrt, [W.k_stat, k_const], [W.k_stat], scale=1.0 / HD, bias=epst[:, 0:1])
                    recip(W.stat[:, 2:3], W.stat[:, 1:2], [W.k_stat], [W.k_stat])
                    stt(otot[:], otot[:], W.stat[:, 2:3], gh_b[:], ALU.mult, ALU.mult, [k_otot, W.k_stat, k_const], [k_otot])
                    tt(DVE, orb[:], otot[:], gst[:, ti, :], ALU.mult, [k_otot, k_gst], [k_orb])
                    pt2, pk2 = next_pt()
                    trn(pt2[:, 0, :], orb[:], [k_orb], [pk2], True)
                    evac(orTst[:, ti * 128:(ti + 1) * 128], pt2[:, 0, :], [pk2], [k_orTst])
            if not final:
                dma(GPD, ofw_d[h, t0:t0 + 512, :].rearrange("(ti p) v -> p ti v", p=128), ost[:], [k_ost], [k_ofwd])
            else:
                dma(GPD, orT_d[h * 128:(h + 1) * 128, t0:t0 + 512], orTst[:], [k_orTst], [k_orTd])

        k_KTd, k_Vd, k_QTd, k_hTd, k_sqd, k_vd, k_gd, k_ofwd, k_gAd, k_gRd, k_orTd, k_oaTd = (Tk() for _ in range(12))
        PBk.append(Tk()); PBk.append(Tk())

        if stop_after >= 1:
            make_hT(x_oth, 0, 0)
            cur["i"] = 0
            for g in range(3):
                ts(DVE, Sst[:].rearrange("p h v -> p (h v)"), Sst[:].rearrange("p h v -> p (h v)"), flg[:, g:g + 1], None,
                   ALU.mult, ALU.bypass, k_S + [k_const], k_S)
                for blk in range(4):
                    t0 = g * NOWN + blk * 512
                    kv_block(rope_oth, t0, NOWN + t0)
                    issue_convs(2)
                    if t0 + 512 < 3 * NOWN:
                        make_hT(x_oth, t0 + 512, 1 - cur["i"])
                    elif stop_after >= 2:
                        make_hT(x_own, 0, 1 - cur["i"], store_hT=0)
                    i_block()
                    state_only_block(g)
                    cur["i"] = 1 - cur["i"]
                stt(Sfw[:].rearrange("p h v -> p (h v)"), Sst[:].rearrange("p h v -> p (h v)"), flg[:, 3 + g:4 + g],
                    Sfw[:].rearrange("p h v -> p (h v)"), ALU.mult, ALU.add, k_S + [k_Sfw, k_const], [k_Sfw])
            ts(DVE, Sst[:].rearrange("p h v -> p (h v)"), Sst[:].rearrange("p h v -> p (h v)"), flg[:, 6:7], None,
               ALU.mult, ALU.bypass, k_S + [k_const], k_S)
            if debug:
                dma(GPD, Sdbg_d[0], Sfw[:].rearrange("p h v -> p (h v)"), [k_Sfw], [Tk()])
                dma(GPD, Sdbg_d[1], Sst[:].rearrange("p h v -> p (h v)"), k_S, [Tk()])

        issue_convs(100)
        if stop_after >= 2:
            for h in range(16):
                evac(Sbf[:, h, :], Sfw[:, h, :], [k_Sfw], [k_Sbf[h]])
            if stop_after < 1:
                make_hT(x_own, 0, cur["i"], store_hT=0)
            for blk in range(4):
                t0 = blk * 512
                hT, k_hT = hTs[cur["i"]], k_hTs[cur["i"]]
                kv_block(rope_own, t0, t0)
                if blk < 3:
                    make_hT(x_own, t0 + 512, 1 - cur["i"], store_hT=t0 + 512)
                for cg in range(4):
                    def qcons(ti, ps, kp, cg=cg):
                        headnorm_rope(ps, kp, 4, gq_b, ropeb[:, ti, :], k_rope, qk_bf[:, 0:512], k_qkbf, W)
                        pt, pk = next_pt()
                        for hh in range(4):
                            trn(pt[:, hh, :], qk_bf[:, hh * 128:(hh + 1) * 128], [k_qkbf], [pk], hh == 3)
                        evac(QTst[:, cg * 4:(cg + 1) * 4, ti * 128:(ti + 1) * 128], pt[:, 0:4, :], [pk], [k_QTst])
                    proj_tm("w_in", 0, C_Q + cg * 512, qcons)
                dma(GPD, QT_d[:, :, t0:t0 + 512].rearrange("h p t -> p h t"), QTst[:], [k_QTst], [k_QTd])
                i_block(store_t0=t0)
                for cg in range(4):
                    def gcons(ti, ps, kp, cg=cg, t0=t0):
                        gev, k_gev = next_gev()
                        act(gev[:], ps, AF.Silu, [kp], [k_gev])
                        dma(GPD, g_d[t0 + ti * 128:t0 + (ti + 1) * 128, cg * 512:(cg + 1) * 512], gev[:], [k_gev], [k_gd])
                    proj_tm("w_in", 0, C_G + cg * 512, gcons)
                for c0, dst, kd in ((C_GA, gA_d, k_gAd), (C_GR, gR_d, k_gRd)):
                    for cg in range(4):
                        def gacons(j, ps, kp, cg=cg, dst=dst, kd=kd, t0=t0):
                            gev, k_gev = next_gev()
                            act(gev[:], ps, AF.Sigmoid, [kp], [k_gev])
                            r0 = cg * 512 + j * 128
                            dma(GPD, dst[r0:r0 + 128, t0:t0 + 512], gev[:], [k_gev], [kd])
                        proj_fm("w_in", 0, c0 + cg * 512, gacons)
                for cg in range(4):
                    wq, wqk = load_w(wbufs, wks, "w_in", 0, 16, C_QR + cg * 512)
                    wf, wfk = load_w(wbufs, wks, "w_in", 0, 16, C_FF + cg * 512)
                    for j in range(4):
                        h = cg * 4 + j
                        ps, kp = next_pb()
                        mm_group(ps[:], kp, [(wq[:, kc, j * 128:(j + 1) * 128], hT[:, kc, :]) for kc in range(16)], [k_hT, wqk])
                        act(sqT[:], ps[:], AF.Silu, [kp], [k_sqT])
                        dma(GPD, sq_d[h, :, t0:t0 + 512], sqT[:], [k_sqT], [k_sqd])
                        ps2, kp2 = next_pb()
                        mm_group(ps2[:], kp2, [(wf[:, kc, j * 128:(j + 1) * 128], hT[:, kc, :]) for kc in range(16)], [k_hT, wfk])
                        own_head(ps2[:], kp2, 0, h, t0, False)
                cur["i"] = 1 - cur["i"]

        if stop_after >= 3:
            for h in range(16):
                evac(Sbf[:, h, :], Sst[:, h, :], [k_S[h]], [k_Sbf[h]])
            for blk in range(3, -1, -1):
                t0 = blk * 512
                cur["i"] = blk % 2
                hT, k_hT = hTs[cur["i"]], k_hTs[cur["i"]]
                dma(SPD, hT[:], hT_d[:, :, t0:t0 + 512], [k_hTd], [k_hT])
                dma(SPD, vblk[:], v_d[t0:t0 + 512, :].rearrange("(ti p) c -> p ti c", p=128), [k_vd], [k_vblk])
                for cg in range(4):
                    wf, wfk = load_w(wbufs, wks, "w_in", 0, 16, C_FB + cg * 512)
                    for j in range(4):
                        h = cg * 4 + j
                        dma(SPD, sqT[:], sq_d[h, :, t0:t0 + 512], [k_sqd], [k_sqT])
                        ps2, kp2 = next_pb()
                        mm_group(ps2[:], kp2, [(wf[:, kc, j * 128:(j + 1) * 128], hT[:, kc, :]) for kc in range(16)], [k_hT, wfk])
                        own_head(ps2[:], kp2, 1, h, t0, True)
        cx.barrier()
        run_block()
    if stop_after < 4:
        return nc, es
    es2 = contextlib.ExitStack()
    with es2:
        KTs = sb("KTs", [128, S], BF16); k_KTs = Tk()
        Vs = sb("Vs", [128, 64, 128], BF16); k_Vs = Tk()
        QTb = [sb(f"QTb{i}", [128, 512], BF16) for i in range(2)]; k_QTb = [Tk(), Tk()]
        Pb = [sb(f"Pb{i}", [128, 512], BF16) for i in range(3)]; k_Pb = [Tk(), Tk(), Tk()]
        ones_bf = sb("ones_bf", [128, 128], BF16)
        rs = sb("rs", [128, 512], F32); k_rs = Tk()
        oab = [sb(f"oab{i}", [128, 512], BF16) for i in range(2)]; k_oab = [Tk(), Tk()]
        cx.op(DVE, lambda e: e.memset(ones_bf[:], 1.0), [], [k_const])
        scale = float(HD) ** -0.5
        it = 0
        for kh in range(NKVH):
            dma(SPD, KTs[:], KT_d[kh], [k_KTd], [k_KTs])
            dma(SPD, Vs[:], V_d[:, kh * 128:(kh + 1) * 128].rearrange("(kt p) c -> p kt c", p=128), [k_Vd], [k_Vs])
            for qh in range(4):
                head = kh * 4 + qh
                for qb in range(4):
                    qt, kq = QTb[it % 2], k_QTb[it % 2]
                    dma(SPD, qt[:], QT_d[head, :, qb * 512:(qb + 1) * 512], [k_QTd], [kq])
                    OT, kOT = PB[2], PBk[2]
                    SM, kSM = PB[3], PBk[3]

                    def smm(kt):
                        mm(PB[kt % 2][:], KTs[:, kt * 128:(kt + 1) * 128], qt[:], True, True, [k_KTs, kq], [PBk[kt % 2]], True)
                    smm(0)
                    smm(1)
                    for kt in range(64):
                        pb_, kpb = Pb[kt % 3], k_Pb[kt % 3]
                        act(pb_[:], PB[kt % 2][:], AF.Exp, [PBk[kt % 2]], [kpb], scale=scale)
                        mm(OT[:], Vs[:, kt, :], pb_[:], kt == 0, kt == 63, [k_Vs, kpb], [kOT], kt == 63)
                        mm(SM[:], ones_bf[:], pb_[:], kt == 0, kt == 63, [k_const, kpb], [kSM], kt == 63)
                        if kt + 2 < 64:
                            smm(kt + 2)
                    recip(rs[:], SM[:], [kSM], [k_rs])
                    ob, kob = oab[it % 2], k_oab[it % 2]
                    tt(DVE, ob[:], OT[:], rs[:], ALU.mult, [kOT, k_rs], [kob])
                    dma(GPD, oaT_d[head * 128:(head + 1) * 128, qb * 512:(qb + 1) * 512], ob[:], [kob], [k_oaTd])
                    it += 1
        cx.barrier()
        run_block()

    if stop_after < 5:
        return nc, es
    es2 = contextlib.ExitStack()
    with es2:
        W = NS()
        W.hb = sb("hb5", [128, D], BF16); W.k_hb = Tk()
        W.stat = sb("stat5", [128, 4], F32); W.k_stat = Tk()
        epst = sb("epst5", [128, 1], F32)
        gmlp_b = sb("gmlp_b", [128, D], F32)
        gple_b = sb("gple_b", [128, D], F32)
        gfin_b = sb("gfin_b", [128, D], F32)
        xres = sb("xres", [128, 4, D], F32); k_x = [Tk() for _ in range(4)]
        actA = sb("actA", [128, 16, 512], BF16); k_actA = Tk()
        actB = sb("actB", [128, 16, 512], BF16); k_actB = Tk()
        gAb = [sb(f"gAb{i}", [128, 512], BF16) for i in range(2)]; k_gAb = [Tk(), Tk()]
        gRb = [sb(f"gRb{i}", [128, 512], BF16) for i in range(2)]; k_gRb = [Tk(), Tk()]
        mixT = sb("mixT", [128, 16, 512], BF16); k_mixT = Tk()
        uT, k_uT = actB, k_actB
        wbufs = [sb(f"wc{i}", [128, 16, 512], BF16) for i in range(2)]; wks = [Tk(), Tk()]
        t1 = sb("t1_5", [128, 512], F32); k_t1 = Tk()
        t2 = sb("t2_5", [128, 512], F32); k_t2 = Tk()
        pin = sb("pin", [128, PLE], F32); k_pin = Tk()
        pbf = sb("pbf", [128, PLE], BF16); k_pbf = Tk()
        pT = sb("pT", [128, 2, 512], BF16); k_pT = Tk()
        outb = sb("outb", [128, D], F32); k_outb = Tk()
        cx.op(DVE, lambda e: e.memset(epst[:], EPS), [], [k_const])
        dma(SPD, gmlp_b[:], g_mlp.partition_broadcast(128), [], [k_const])
        dma(SPD, gple_b[:], g_ple.partition_broadcast(128), [], [k_const])
        dma(SPD, gfin_b[:], g_final.partition_broadcast(128), [], [k_const])

        def load_act(dst, kdst, src, ksrc, t0):
            dma(GPD, dst[:], src[:, t0:t0 + 512].rearrange("(kc p) t -> p kc t", p=128), [ksrc], [kdst])

        for blk in range(4):
            t0 = blk * 512
            for ti in range(4):
                dma(SPD, xres[:, ti, :], x_own[t0 + ti * 128:t0 + (ti + 1) * 128, :], [], [k_x[ti]])
            load_act(actA, k_actA, oaT_d, k_oaTd, t0)
            load_act(actB, k_actB, orT_d, k_orTd, t0)
            for cg in range(4):
                wa, wak = load_w(wbufs, wks, "w_o_attn", 0, 16, cg * 512)
                wr, wrk = load_w(wbufs, wks, "w_o_hgrn", 0, 16, cg * 512)
                for j in range(4):
                    dc = cg * 4 + j
                    pa, kpa = next_pb()
                    mm_group(pa[:], kpa, [(wa[:, kc, j * 128:(j + 1) * 128], actA[:, kc, :]) for kc in range(16)], [k_actA, wak])
                    pr, kpr = next_pb()
                    mm_group(pr[:], kpr, [(wr[:, kc, j * 128:(j + 1) * 128], actB[:, kc, :]) for kc in range(16)], [k_actB, wrk])
                    ga, kga, gr, kgr = gAb[dc % 2], k_gAb[dc % 2], gRb[dc % 2], k_gRb[dc % 2]
                    dma(SPD, ga[:], gA_d[dc * 128:(dc + 1) * 128, t0:t0 + 512], [k_gAd], [kga])
                    dma(SPD, gr[:], gR_d[dc * 128:(dc + 1) * 128, t0:t0 + 512], [k_gRd], [kgr])
                    tt(DVE, t1[:], pa[:], ga[:], ALU.mult, [kpa, kga], [k_t1])
                    tt(DVE, t2[:], pr[:], gr[:], ALU.mult, [kpr, kgr], [k_t2])
                    tt(POOL, mixT[:, dc, :], t1[:], t2[:], ALU.add, [k_t1, k_t2], [k_mixT])
            for cg in range(4):
                wo, wok = load_w(wbufs, wks, "w_out", 0, 16, cg * 512)
                for ti in range(4):
                    ps, kp = next_pb()
                    mm_group(ps[:], kp, [(mixT[:, kc, ti * 128:(ti + 1) * 128], wo[:, kc, :]) for kc in range(16)], [k_mixT, wok])
                    tt(DVE, xres[:, ti, cg * 512:(cg + 1) * 512], xres[:, ti, cg * 512:(cg + 1) * 512], ps[:], ALU.add,
                       [k_x[ti], kp], [k_x[ti]])
            for ti in range(4):
                norm_tile(xres[:, ti, :], k_x[ti], gmlp_b, actA, k_actA, ti, W)
            for fq in range(4):
                for fg in range(4):
                    wu, wuk = load_w(wbufs, wks, "w_up", 0, 16, fq * 2048 + fg * 512)
                    for j in range(4):
                        ps, kp = next_pb()
                        mm_group(ps[:], kp, [(wu[:, kc, j * 128:(j + 1) * 128], actA[:, kc, :]) for kc in range(16)], [k_actA, wuk])
                        act(t1[:], ps[:], AF.Square, [kp], [k_t1])
                        stt(uT[:, fg * 4 + j, :], ps[:], 0.0, t1[:], ALU.is_gt, ALU.mult, [kp, k_t1], [k_uT])
                for cg in range(4):
                    wd, wdk = load_w(wbufs, wks, "w_down", fq * 2048, 16, cg * 512)
                    for ti in range(4):
                        ps, kp = next_pb()
                        mm_group(ps[:], kp, [(uT[:, kc, ti * 128:(ti + 1) * 128], wd[:, kc, :]) for kc in range(16)], [k_uT, wdk])
                        tt(DVE, xres[:, ti, cg * 512:(cg + 1) * 512], xres[:, ti, cg * 512:(cg + 1) * 512], ps[:], ALU.add,
                           [k_x[ti], kp], [k_x[ti]])
            for ti in range(4):
                norm_tile(xres[:, ti, :], k_x[ti], gple_b, actA, k_actA, ti, W)
                dma(SPD, pin[:], p_own[t0 + ti * 128:t0 + (ti + 1) * 128, :], [], [k_pin])
                evac(pbf[:], pin[:], [k_pin], [k_pbf], eng=DVE)
                pt, pk = next_pt()
                trn(pt[:, 0, :], pbf[:, 0:128], [k_pbf], [pk], False)
                trn(pt[:, 1, :], pbf[:, 128:256], [k_pbf], [pk], True)
                evac(pT[:, :, ti * 128:(ti + 1) * 128], pt[:, 0:2, :], [pk], [k_pT])
            for cg in range(4):
                wg, wgk = load_w(wbufs, wks, "w_ple_gate", 0, 16, cg * 512)
                wp, wpk = load_w(wbufs, wks, "w_ple", 0, 2, cg * 512)
                for ti in range(4):
                    ps, kp = next_pb()
                    mm_group(ps[:], kp, [(actA[:, kc, ti * 128:(ti + 1) * 128], wg[:, kc, :]) for kc in range(16)], [k_actA, wgk])
                    act(t1[:], ps[:], AF.Sigmoid, [kp], [k_t1])
                    ps2, kp2 = next_pb()
                    mm_group(ps2[:], kp2, [(pT[:, kc, ti * 128:(ti + 1) * 128], wp[:, kc, :]) for kc in range(2)], [k_pT, wpk])
                    tt(DVE, t2[:], ps2[:], t1[:], ALU.mult, [kp2, k_t1], [k_t2])
                    tt(POOL, xres[:, ti, cg * 512:(cg + 1) * 512], xres[:, ti, cg * 512:(cg + 1) * 512], t2[:], ALU.add,
                       [k_x[ti], k_t2], [k_x[ti]])
            for ti in range(4):
                act(W.hb[:], xres[:, ti, :], AF.Square, [k_x[ti]], [W.k_hb, W.k_stat], accum_out=W.stat[:, 0:1])
                act(W.stat[:, 1:2], W.stat[:, 0:1], AF.Sqrt, [W.k_stat, k_const], [W.k_stat], scale=1.0 / D, bias=epst[:, 0:1])
                recip(W.stat[:, 2:3], W.stat[:, 1:2], [W.k_stat], [W.k_stat])
                stt(outb[:], xres[:, ti, :], W.stat[:, 2:3], gfin_b[:], ALU.mult, ALU.mult, [k_x[ti], W.k_stat, k_const], [k_outb])
                dma(GPD, out[t0 + ti * 128:t0 + (ti + 1) * 128, :], outb[:], [k_outb], [k_out])
        cx.barrier()
        run_block()
    return nc, es


def _rope_table(pos):
    inv = (10000.0 ** (-np.arange(0, 64, 2, dtype=np.float32) / np.float32(64))).astype(np.float32)
    row = (pos // 64).astype(np.float32)[:, None] * inv[None, :]
    col = (pos % 64).astype(np.float32)[:, None] * inv[None, :]
    return np.concatenate([np.cos(row), np.cos(col), np.sin(row), np.sin(col)], axis=1).astype(np.float32)


def _consts():
    s = np.arange(128)[:, None]
    t = np.arange(128)[None, :]
    same = (s // 64) == (t // 64)
    tt_ = np.arange(512)
    rf = np.ones((128, 512), np.float32); rf[:, tt_ % 64 == 0] = 0.0
    rb = np.ones((128, 512), np.float32); rb[:, tt_ % 64 == 63] = 0.0
    return {
        "c_ident": np.eye(128, dtype=np.float32).astype(ml_dtypes.bfloat16),
        "c_maskF": (same & (s <= t)).astype(np.float32),
        "c_maskB": (same & (s >= t)).astype(np.float32),
        "c_resetF": rf, "c_resetB": rb,
    }


def make_in_maps(x, p, g_mix, w_in, g_q, g_k, w_o_attn, hgrn_lb, g_hgrn, w_o_hgrn, w_out,
                 g_mlp, w_up, w_down, g_ple, w_ple_gate, w_ple, g_final):
    f = lambda a: np.ascontiguousarray(np.asarray(a, dtype=np.float32))
    x = f(x); p = f(p); w_in0 = f(w_in[0]); hl = f(hgrn_lb)
    shared = dict(
        w_in=w_in0, g_mix=f(g_mix[0]), g_q=f(g_q[0]), g_k=f(g_k[0]), g_hgrn=f(g_hgrn[0]), g_mlp=f(g_mlp[0]),
        g_ple=f(g_ple[0]), g_final=f(g_final), w_o_attn=f(w_o_attn[0]), w_o_hgrn=f(w_o_hgrn[0]), w_out=f(w_out[0]),
        w_up=f(w_up[0]), w_down=f(w_down[0]), w_ple_gate=f(w_ple_gate[0]), w_ple=f(w_ple[0]),
        lbown=np.ascontiguousarray(hl.transpose(1, 0, 2).reshape(2, 2, 16, 128).transpose(3, 0, 1, 2)),
    )
    shared.update(_consts())
    wf = [np.ascontiguousarray(w_in0[:, C_FF:C_FF + D]), np.ascontiguousarray(w_in0[:, C_FB:C_FB + D])]
    maps = []
    for c in range(8):
        b, qi = c // 4, c % 4
        pos_own = np.arange(qi * NOWN, (qi + 1) * NOWN)
        quarters, dirs, poss = [], [], []
        for g in range(3):
            if g < qi:
                qd, d_ = g, 0
                pos = np.arange(qd * NOWN, (qd + 1) * NOWN)
            else:
                qd, d_ = 3 - (g - qi), 1
                pos = np.arange(qd * NOWN, (qd + 1) * NOWN)[::-1]
            dirs.append(d_); poss.append(pos)
        pos_oth = np.concatenate(poss)
        fl = np.zeros((128, 8), np.float32)
        for g in range(3):
            fl[:, g] = 0.0 if g == qi else 1.0
            fl[:, 3 + g] = 1.0 if g == qi - 1 else 0.0
        fl[:, 6] = 1.0 if qi < 3 else 0.0
        m = dict(shared)
        m.update(
            x_own=np.ascontiguousarray(x[b, pos_own]), x_oth=np.ascontiguousarray(x[b, pos_oth]),
            rope_own=_rope_table(pos_own), rope_oth=_rope_table(pos_oth),
            p_own=np.ascontiguousarray(p[0, b, pos_own]),
            w_fsel=np.concatenate([wf[d_] for d_ in dirs], axis=0),
            lbsel=np.ascontiguousarray(np.stack([hl[:, d_, :] for d_ in dirs], axis=0).reshape(3, 2, 16, 128).transpose(3, 0, 1, 2)),
            flags=fl,
        )
        maps.append(m)
    return maps


_NC_CACHE = {}


def kernel(**inputs):
    maps = make_in_maps(**inputs)
    if "nc" not in _NC_CACHE:
        _NC_CACHE["nc"] = build_nc()
    nc, _es = _NC_CACHE["nc"]
    res = run_bass_kernel_spmd(nc, maps, core_ids=list(range(8)))
    outp = np.empty((2, S, D), np.float32)
    for c in range(8):
        b, qi = c // 4, c % 4
        outp[b, qi * NOWN:(qi + 1) * NOWN] = res.results[c]["out"]
    return outp
```
